# Optimizing a Trainium2 kernel written in Bass

```python
import math
import jax, jax.numpy as jnp
from jax import lax
import numpy as np

D_MODEL = 1024
BATCH = 16
SEQ = 4096
DEPTH = 4

N_META = 16
GRID_W = 64
HEAD_DIM = 64
BLOCK = 128
A_HEADS = 4
B_HEADS = 8
NA_ROWS_MAX = 8
NA_COLS = 16
C_HEADS = 8
C_KV_HEADS = 2
WINDOW = 128
T5_BUCKETS = 32
T5_MAX_DIST = 128
D_FF = 2816
N_BRANCH = 3
BRANCH_W = 512
A_COLS = 3 * A_HEADS * 2 * HEAD_DIM
B_COLS = 3 * B_HEADS * HEAD_DIM
C_COLS = (C_HEADS + 2 * C_KV_HEADS) * HEAD_DIM
IN_COLS = A_COLS + B_COLS + C_COLS
EPS = 1e-6
NEG = -1e30

kernel_name = "hybrid_gated_diff_natten_swa_encoder"


def rms_norm(x, g):
    xf = x.astype(jnp.float32)
    y = xf * lax.rsqrt(jnp.mean(xf * xf, axis=-1, keepdims=True) + EPS)
    return (y * g.astype(jnp.float32)).astype(x.dtype)


def swiglu(x, w_in, w_out):
    g, u = jnp.split(x @ w_in, 2, axis=-1)
    return (jax.nn.silu(g) * u) @ w_out


def t5_bucket(rel):
    nb = T5_BUCKETS // 2
    max_exact = nb // 2
    ret = jnp.where(rel > 0, nb, 0)
    n = jnp.abs(rel)
    nf = jnp.maximum(n, 1).astype(jnp.float32)
    large = max_exact + (jnp.log(nf / max_exact) / math.log(T5_MAX_DIST / max_exact)
                         * (nb - max_exact)).astype(jnp.int32)
    large = jnp.minimum(large, nb - 1)
    return ret + jnp.where(n < max_exact, n, large)


def t5_bias(table, rel):
    return jnp.moveaxis(table[t5_bucket(rel)], -1, 0).astype(jnp.float32)


def sink_softmax(s, sink):
    m = jnp.maximum(jnp.max(s, axis=-1, keepdims=True), sink)
    e = jnp.exp(s - m)
    return e / (jnp.sum(e, axis=-1, keepdims=True) + jnp.exp(sink - m))


def diff_attention(q, k, v, lam, lam_init, subln_g, table):
    bsz, L, H, _, dh = q.shape
    S = L - N_META
    nblk = S // BLOCK
    scale = dh ** -0.5
    kpos = jnp.arange(L)

    def attend(qb, qpos):
        s = jnp.einsum("bqhmd,bkhmd->bhmqk", qb, k).astype(jnp.float32) * scale
        s = s + t5_bias(table, kpos[None, :] - qpos[:, None])[None, :, None]
        p = jax.nn.softmax(s, axis=-1)
        a = p[:, :, 0] - lam * p[:, :, 1]
        return jnp.einsum("bhqk,bkhe->bqhe", a.astype(v.dtype), v)

    o_meta = attend(q[:, :N_META], jnp.arange(N_META))
    q_blocks = jnp.swapaxes(q[:, N_META:].reshape(bsz, nblk, BLOCK, H, 2, dh), 0, 1)
    pos_blocks = N_META + jnp.arange(S).reshape(nblk, BLOCK)
    o_real = lax.map(lambda a: attend(a[0], a[1]), (q_blocks, pos_blocks))
    o_real = jnp.swapaxes(o_real, 0, 1).reshape(bsz, S, H, 2 * dh)
    o = jnp.concatenate([o_meta, o_real], axis=1)
    o = rms_norm(o, subln_g) * (1.0 - lam_init)
    return o.reshape(bsz, L, H * 2 * dh)


def neighborhood_attention(q, k, v, rpb):
    bsz, L, H, dh = q.shape
    S = L - N_META
    rows = S // GRID_W
    wr = min(NA_ROWS_MAX, rows)
    scale = dh ** -0.5
    qm, km, vm = q[:, :N_META], k[:, :N_META], v[:, :N_META]
    qg = q[:, N_META:].reshape(bsz, rows, GRID_W, H, dh)
    kg = k[:, N_META:].reshape(bsz, rows, GRID_W, H, dh)
    vg = v[:, N_META:].reshape(bsz, rows, GRID_W, H, dh)

    cols = jnp.arange(GRID_W)
    cstart = jnp.clip(cols - NA_COLS // 2, 0, GRID_W - NA_COLS)
    col_ok = (cols[None, :] >= cstart[:, None]) & (cols[None, :] < cstart[:, None] + NA_COLS)
    col_idx = jnp.clip(cols[None, :] - cols[:, None] + NA_COLS - 1, 0, 2 * NA_COLS - 2)
    rb_cols = rpb[:, :, col_idx]

    def row(args):
        q_row, r = args
        rs = jnp.clip(r - wr // 2, 0, rows - wr)
        k_nb = lax.dynamic_slice_in_dim(kg, rs, wr, axis=1)
        v_nb = lax.dynamic_slice_in_dim(vg, rs, wr, axis=1)
        s = jnp.einsum("bchd,bxyhd->bhcxy", q_row, k_nb).astype(jnp.float32) * scale
        row_off = rs + jnp.arange(wr) - r + NA_ROWS_MAX - 1
        bias = jnp.transpose(rb_cols[:, row_off], (0, 2, 1, 3)).astype(jnp.float32)
        s = jnp.where(col_ok[:, None, :], s + bias[None], NEG).reshape(bsz, H, GRID_W, wr * GRID_W)
        sm = jnp.einsum("bchd,bmhd->bhcm", q_row, km).astype(jnp.float32) * scale
        p = jax.nn.softmax(jnp.concatenate([sm, s], axis=-1), axis=-1).astype(v.dtype)
        p_nb = p[..., N_META:].reshape(bsz, H, GRID_W, wr, GRID_W)
        return (jnp.einsum("bhcm,bmhd->bchd", p[..., :N_META], vm)
                + jnp.einsum("bhcxy,bxyhd->bchd", p_nb, v_nb))

    o_real = lax.map(row, (jnp.swapaxes(qg, 0, 1), jnp.arange(rows)))
    o_real = jnp.swapaxes(o_real, 0, 1).reshape(bsz, S, H, dh)

    k_org = kg[:, :wr, :NA_COLS].reshape(bsz, wr * NA_COLS, H, dh)
    v_org = vg[:, :wr, :NA_COLS].reshape(bsz, wr * NA_COLS, H, dh)
    k_mq = jnp.concatenate([km, k_org], axis=1)
    v_mq = jnp.concatenate([vm, v_org], axis=1)
    s_m = jnp.einsum("bqhd,bkhd->bhqk", qm, k_mq).astype(jnp.float32) * scale
    p_m = jax.nn.softmax(s_m, axis=-1).astype(v.dtype)
    o_meta = jnp.einsum("bhqk,bkhd->bqhd", p_m, v_mq)
    return jnp.concatenate([o_meta, o_real], axis=1).reshape(bsz, L, H * dh)


def window_gqa(q, k, v, sink, table):
    bsz, L, HQ, dh = q.shape
    KV = k.shape[2]
    G = HQ // KV
    S = L - N_META
    nblk = S // BLOCK
    scale = dh ** -0.5
    q = q.reshape(bsz, L, KV, G, dh)
    qm, km, vm = q[:, :N_META], k[:, :N_META], v[:, :N_META]
    sink_b = sink.astype(jnp.float32).reshape(KV, G, 1, 1)
    mpos = jnp.arange(N_META)

    pad = ((0, 0), (BLOCK, BLOCK), (0, 0), (0, 0))
    kp = jnp.pad(k[:, N_META:], pad)
    vp = jnp.pad(v[:, N_META:], pad)
    qi = jnp.arange(BLOCK)
    ki = jnp.arange(3 * BLOCK) - BLOCK
    rel = ki[None, :] - qi[:, None]
    band_bias = t5_bias(table, rel).reshape(KV, G, BLOCK, 3 * BLOCK)

    def block(args):
        qb, j = args
        kb = lax.dynamic_slice_in_dim(kp, j * BLOCK, 3 * BLOCK, axis=1)
        vb = lax.dynamic_slice_in_dim(vp, j * BLOCK, 3 * BLOCK, axis=1)
        kabs = j * BLOCK + ki
        valid = (jnp.abs(rel) <= WINDOW) & ((kabs >= 0) & (kabs < S))[None, :]
        s_b = jnp.einsum("bqkgd,bjkd->bkgqj", qb, kb).astype(jnp.float32) * scale + band_bias
        s_b = jnp.where(valid, s_b, NEG)
        qpos = N_META + j * BLOCK + qi
        s_m = (jnp.einsum("bqkgd,bmkd->bkgqm", qb, km).astype(jnp.float32) * scale
               + t5_bias(table, mpos[None, :] - qpos[:, None]).reshape(KV, G, BLOCK, N_META))
        p = sink_softmax(jnp.concatenate([s_m, s_b], axis=-1), sink_b).astype(v.dtype)
        return (jnp.einsum("bkgqm,bmkd->bqkgd", p[..., :N_META], vm)
                + jnp.einsum("bkgqj,bjkd->bqkgd", p[..., N_META:], vb))

    q_blocks = jnp.swapaxes(q[:, N_META:].reshape(bsz, nblk, BLOCK, KV, G, dh), 0, 1)
    o_real = lax.map(block, (q_blocks, jnp.arange(nblk)))
    o_real = jnp.swapaxes(o_real, 0, 1).reshape(bsz, S, KV, G, dh)

    k0 = k[:, N_META:N_META + BLOCK]
    v0 = v[:, N_META:N_META + BLOCK]
    rel0 = (N_META + jnp.arange(BLOCK))[None, :] - mpos[:, None]
    s_mm = (jnp.einsum("bqkgd,bmkd->bkgqm", qm, km).astype(jnp.float32) * scale
            + t5_bias(table, mpos[None, :] - mpos[:, None]).reshape(KV, G, N_META, N_META))
    s_m0 = (jnp.einsum("bqkgd,bjkd->bkgqj", qm, k0).astype(jnp.float32) * scale
            + t5_bias(table, rel0).reshape(KV, G, N_META, BLOCK))
    s_m0 = jnp.where(rel0 <= WINDOW, s_m0, NEG)
    p_m = sink_softmax(jnp.concatenate([s_mm, s_m0], axis=-1), sink_b).astype(v.dtype)
    o_meta = (jnp.einsum("bkgqm,bmkd->bqkgd", p_m[..., :N_META], vm)
              + jnp.einsum("bkgqj,bjkd->bqkgd", p_m[..., N_META:], v0))
    return jnp.concatenate([o_meta, o_real], axis=1).reshape(bsz, L, HQ * dh)


def token_mixer(xn, w_in, lam_q1, lam_k1, lam_q2, lam_k2, subln_g, rpb, sink, t5_table,
                w_branch, w_gate, w_out, lam_init):
    bsz, L, _ = xn.shape
    proj = xn @ w_in
    pa, pb, pc = jnp.split(proj, [A_COLS, A_COLS + B_COLS], axis=-1)

    qa, ka, va = jnp.split(pa, 3, axis=-1)
    qa = qa.reshape(bsz, L, A_HEADS, 2, HEAD_DIM)
    ka = ka.reshape(bsz, L, A_HEADS, 2, HEAD_DIM)
    va = va.reshape(bsz, L, A_HEADS, 2 * HEAD_DIM)
    f32 = jnp.float32
    lam = (jnp.exp(jnp.sum(lam_q1.astype(f32) * lam_k1.astype(f32)))
           - jnp.exp(jnp.sum(lam_q2.astype(f32) * lam_k2.astype(f32))) + lam_init)
    ya = diff_attention(qa, ka, va, lam, lam_init, subln_g, t5_table[:, :A_HEADS])

    qb, kb, vb = [t.reshape(bsz, L, B_HEADS, HEAD_DIM) for t in jnp.split(pb, 3, axis=-1)]
    yb = neighborhood_attention(qb, kb, vb, rpb)

    qc, kc, vc = jnp.split(pc, [C_HEADS * HEAD_DIM, (C_HEADS + C_KV_HEADS) * HEAD_DIM], axis=-1)
    qc = qc.reshape(bsz, L, C_HEADS, HEAD_DIM)
    kc = kc.reshape(bsz, L, C_KV_HEADS, HEAD_DIM)
    vc = vc.reshape(bsz, L, C_KV_HEADS, HEAD_DIM)
    yc = window_gqa(qc, kc, vc, sink, t5_table[:, A_HEADS:])

    merged = (jax.nn.sigmoid(xn @ w_gate[0]) * (ya @ w_branch[0])
              + jax.nn.sigmoid(xn @ w_gate[1]) * (yb @ w_branch[1])
              + jax.nn.sigmoid(xn @ w_gate[2]) * (yc @ w_branch[2]))
    return merged @ w_out


def setup_inputs(seed: int = 0) -> dict:
    key = jax.random.key(seed)
    ks = jax.random.split(key, 24)
    f32 = jnp.float32
    D = D_MODEL

    def nrm(k, shape, scale):
        return jax.random.normal(k, shape, f32) * scale

    def gain(k, shape):
        return 1.0 + 0.05 * jax.random.normal(k, shape, f32)

    return {
        "x": nrm(ks[0], (BATCH, SEQ, D), 1.0),
        "meta_tokens": nrm(ks[1], (N_META, D), 1.0),
        "t5_table": nrm(ks[2], (T5_BUCKETS, A_HEADS + C_HEADS), 0.5),
        "norm_ffn1": gain(ks[3], (DEPTH, D)),
        "w_ffn1_in": nrm(ks[4], (DEPTH, D, 2 * D_FF), D ** -0.5),
        "w_ffn1_out": nrm(ks[5], (DEPTH, D_FF, D), D_FF ** -0.5),
        "norm_mix": gain(ks[6], (DEPTH, D)),
        "w_in": nrm(ks[7], (DEPTH, D, IN_COLS), D ** -0.5),
        "lambda_q1": nrm(ks[8], (DEPTH, HEAD_DIM), 0.1),
        "lambda_k1": nrm(ks[9], (DEPTH, HEAD_DIM), 0.1),
        "lambda_q2": nrm(ks[10], (DEPTH, HEAD_DIM), 0.1),
        "lambda_k2": nrm(ks[11], (DEPTH, HEAD_DIM), 0.1),
        "subln_gain": gain(ks[12], (DEPTH, 2 * HEAD_DIM)),
        "natten_rpb": nrm(ks[13], (DEPTH, B_HEADS, 2 * NA_ROWS_MAX - 1, 2 * NA_COLS - 1), 0.5),
        "sink_logits": nrm(ks[14], (DEPTH, C_HEADS), 0.5),
        "w_branch": nrm(ks[15], (DEPTH, N_BRANCH, BRANCH_W, D), BRANCH_W ** -0.5),
        "w_gate": nrm(ks[16], (DEPTH, N_BRANCH, D, D), D ** -0.5),
        "w_out": nrm(ks[17], (DEPTH, D, D), D ** -0.5),
        "norm_ffn2": gain(ks[18], (DEPTH, D)),
        "w_ffn2_in": nrm(ks[19], (DEPTH, D, 2 * D_FF), D ** -0.5),
        "w_ffn2_out": nrm(ks[20], (DEPTH, D_FF, D), D_FF ** -0.5),
        "final_norm": gain(ks[21], (D,)),
    }


def reference(x, meta_tokens, t5_table, norm_ffn1, w_ffn1_in, w_ffn1_out, norm_mix, w_in,
              lambda_q1, lambda_k1, lambda_q2, lambda_k2, subln_gain, natten_rpb, sink_logits,
              w_branch, w_gate, w_out, norm_ffn2, w_ffn2_in, w_ffn2_out, final_norm):
    bsz = x.shape[0]
    meta = jnp.broadcast_to(meta_tokens[None].astype(x.dtype), (bsz, N_META, D_MODEL))
    h = jnp.concatenate([meta, x], axis=1)
    for l in range(DEPTH):
        lam_init = 0.8 - 0.6 * math.exp(-0.3 * l)
        h = h + 0.5 * swiglu(rms_norm(h, norm_ffn1[l]), w_ffn1_in[l], w_ffn1_out[l])
        h = h + token_mixer(rms_norm(h, norm_mix[l]), w_in[l], lambda_q1[l], lambda_k1[l],
                            lambda_q2[l], lambda_k2[l], subln_gain[l], natten_rpb[l], sink_logits[l],
                            t5_table, w_branch[l], w_gate[l], w_out[l], lam_init)
        h = h + 0.5 * swiglu(rms_norm(h, norm_ffn2[l]), w_ffn2_in[l], w_ffn2_out[l])
    return rms_norm(h, final_norm)[:, N_META:]
```

```python
import math
import numpy as np
import concourse.bass as bass
import concourse.mybir as mybir
from concourse.bass_utils import run_bass_kernel_spmd

F32 = mybir.dt.float32
BF16 = mybir.dt.bfloat16
AF = mybir.ActivationFunctionType
ALU = mybir.AluOpType

D = 1024; DFF = 2816; NM = 16; SEQ = 4096; L = SEQ + NM; DEPTH = 4
NVC = 4 * 129 + 8 * 65 + 2 * 65
VA0, VB0, VC0 = 0, 516, 1036
EPS = 1e-6
NEGM = -30000.0
NQK = 22


class Op:
    __slots__ = ("eng", "fn", "deps", "dma", "inc", "ms", "lane", "tgt", "lc")


class Sched:
    ENGS = ("pe", "act", "dve", "pool", "sp")

    def __init__(self):
        self.ops = {e: [] for e in self.ENGS}
        self.lastw = {}
        self.rd_eng = {}
        self.rd_dma = {}
        self.n = 0

    def add(self, eng, fn, reads=(), writes=(), dma=False):
        op = Op()
        op.lc = eng
        if eng == "cast":
            eng = "pool"
        op.eng = eng; op.fn = fn; op.dma = dma; op.inc = False; op.ms = 0; op.lane = 0; op.tgt = 0
        deps = []
        for r in reads:
            w = self.lastw.get(r)
            if w is not None:
                deps.append(w)
        for r in writes:
            w = self.lastw.get(r)
            if w is not None:
                deps.append(w)
            re = self.rd_eng.get(r)
            if re:
                deps.extend(re.values())
            rdm = self.rd_dma.get(r)
            if rdm:
                deps.extend(rdm)
        fd = []
        seen = set()
        for d in deps:
            if d is op or id(d) in seen:
                continue
            seen.add(id(d))
            if (not d.dma) and (not dma) and d.eng == "pe" and eng == "pe":
                continue
            if not d.dma:
                d.inc = True
            fd.append(d)
        op.deps = fd
        for r in reads:
            if dma:
                self.rd_dma.setdefault(r, []).append(op)
            else:
                self.rd_eng.setdefault(r, {})[eng] = op
        for r in writes:
            self.lastw[r] = op
            self.rd_eng[r] = {}
            self.rd_dma[r] = []
        self.ops[eng].append(op)
        self.n += 1
        return op

    def emit(self, nc, block, sems, lanes):
        for e in self.ENGS:
            c = 0
            for op in self.ops[e]:
                if op.inc and not op.dma:
                    c += 1
                    op.ms = c
        lane_cnt = {}
        li = {}
        for e in self.ENGS:
            for op in self.ops[e]:
                if op.dma:
                    ln = lanes[op.lc]
                    i = li.get(op.lc, 0)
                    li[op.lc] = i + 1
                    s = ln[i % len(ln)]
                    lane_cnt[s] = lane_cnt.get(s, 0) + 1
                    op.lane = s
                    op.tgt = 16 * lane_cnt[s]
        final = dict(lane_cnt)

        def run(e, eng):
            seen = {}

            def wait(sem, val):
                if seen.get(id(sem), 0) < val:
                    eng.wait_ge(sem, val)
                    seen[id(sem)] = val

            for op in self.ops[e]:
                for d in op.deps:
                    if d.dma:
                        wait(d.lane, d.tgt)
                    else:
                        wait(sems[d.eng], d.ms)
                if op.dma:
                    if op.tgt > 16:
                        wait(op.lane, op.tgt - 16)
                    op.fn(eng).then_inc(op.lane, 16)
                elif op.inc:
                    op.fn(eng).then_inc(sems[e], 1)
                else:
                    op.fn(eng)
            if e == "sp":
                for s, c in final.items():
                    wait(s, 16 * c)

        @block.tensor
        def _(eng):
            run("pe", eng)

        @block.scalar
        def _(eng):
            run("act", eng)

        @block.vector
        def _(eng):
            run("dve", eng)

        @block.gpsimd
        def _(eng):
            run("pool", eng)

        @block.sync
        def _(eng):
            run("sp", eng)


class Tile:
    pass


def make_tiles(nseq):
    tiles = []
    for s in range(nseq):
        for i in range(8):
            t = Tile()
            t.seq = s; t.idx = i
            if i == 0:
                t.pos0 = 0; t.n = 528
                t.blocks = [(0, 16)] + [(16 + 128 * j, 128) for j in range(4)]
                t.qb = [-1, 0, 1, 2, 3]
                t.chunks = [(0, 272), (272, 256)]
                t.groups = [[0, 1, 2], [3, 4]]
            else:
                t.pos0 = 16 + 512 * i; t.n = 512
                t.blocks = [(128 * j, 128) for j in range(4)]
                t.qb = [4 * i + j for j in range(4)]
                t.chunks = [(0, 512)]
                t.groups = [[0, 1, 2, 3]]
            t.row0 = s * L + t.pos0
            tiles.append(t)
    return tiles


def t5_bucket_np(rel):
    nb = 16; max_exact = 8
    ret = np.where(rel > 0, nb, 0)
    n = np.abs(rel)
    nf = np.maximum(n, 1).astype(np.float32)
    large = max_exact + (np.log(nf / max_exact) / np.float32(math.log(128 / max_exact))
                         * (nb - max_exact)).astype(np.int32)
    large = np.minimum(large, nb - 1)
    return ret + np.where(n < max_exact, n, large)


class BiasTab:
    def __init__(self):
        self.keys = {}
        self.specs = []

    def get(self, mixer, h, kpos0, nk, qpos0, nq):
        lo = kpos0 - (qpos0 + nq - 1); hi = (kpos0 + nk - 1) - qpos0
        kmeta = kpos0 < NM
        if lo >= 91:
            d = "P"
        elif hi <= -91:
            d = "N"
        else:
            d = kpos0 - qpos0
        if mixer == "C" and not kmeta:
            d = kpos0 - qpos0
        key = (mixer, h, nk, nq, d, kmeta if mixer == "C" else False)
        if key not in self.keys:
            self.keys[key] = len(self.specs)
            self.specs.append((mixer, h, kpos0, nk, qpos0, nq))
        return self.keys[key]

    def fill(self, t5):
        out = np.zeros((128, len(self.specs), 128), np.float32)
        for i, (mixer, h, kpos0, nk, qpos0, nq) in enumerate(self.specs):
            kp = kpos0 + np.arange(nk)[:, None]; qp = qpos0 + np.arange(nq)[None, :]
            rel = kp - qp
            b = t5_bucket_np(rel)
            if mixer == "A":
                v = t5[b, h]
            else:
                v = t5[b, 4 + h]
                vis = (kp < NM) | (np.abs(rel) <= 128)
                v = np.where(vis, v, np.float32(NEGM))
            out[:nk, i, :nq] = v
        return out


def build_atoms(rpb):
    cols = np.arange(64)
    cstart = np.clip(cols - 8, 0, 48)
    ok = (cols[None, :] >= cstart[:, None]) & (cols[None, :] < cstart[:, None] + 16)
    cidx = np.clip(cols[None, :] - cols[:, None] + 15, 0, 30)
    out = np.full((DEPTH, 128, 122, 64), np.float32(NEGM), np.float32)
    for l in range(DEPTH):
        for hd in range(8):
            for dr in range(15):
                g = rpb[l, hd, dr][cidx]
                a = np.where(ok, g, np.float32(NEGM)).T
                out[l, 0:64, hd * 15 + dr, :] = a
                out[l, 64:128, hd * 15 + dr, :] = a
        mq = np.full((64, 64), np.float32(NEGM), np.float32)
        mq[:16, :16] = 0.0
        out[l, 0:64, 121, :] = mq
        out[l, 64:128, 121, :] = mq
    return out


def build_program(nseq=2, depth=DEPTH, dbg=None):
    dbg = dbg or {}
    stages = dbg.get('stages', {'attnA', 'attnB', 'attnC', 'merge', 'ffn2', 'ffn1', 'proj'})
    nc = bass.Bass("TRN2", target_bir_lowering=False)
    S = Sched()
    tab = BiasTab()
    tiles = make_tiles(nseq)
    NTOK = nseq * L

    def din(name, shape, dt=F32):
        return nc.dram_tensor(name, list(shape), dt, kind="ExternalInput").ap()

    def dscr(name, shape, dt):
        return nc.dram_tensor(name, list(shape), dt, kind="Internal").ap()

    x_in = din("x", [nseq * SEQ, D])
    meta_in = din("meta", [NM, D])
    w_f1i = din("w_ffn1_in", [DEPTH * D, 2 * DFF]); w_f1o = din("w_ffn1_out", [DEPTH * DFF, D])
    w_f2i = din("w_ffn2_in", [DEPTH * D, 2 * DFF]); w_f2o = din("w_ffn2_out", [DEPTH * DFF, D])
    w_inp = din("w_in", [DEPTH * D, 3840]); w_gat = din("w_gate", [DEPTH * 3 * D, D])
    w_brn = din("w_branch", [DEPTH * 3 * 512, D]); w_out = din("w_out", [DEPTH * D, D])
    gcols_in = din("gcols", [128, 3 * DEPTH * 8])
    gfin_in = din("final_norm", [1, D])
    lam_in = din("lam4", [1, DEPTH * 4 * 64])
    subg_in = din("subln", [1, DEPTH * 128])
    sink_in = din("sink", [1, DEPTH * 8])
    t5_in = din("t5flat", [1, 384])
    ident_in = din("ident", [128, 128])
    sel_in = din("sel", [128, 256])
    atoms_in = din("atoms", [DEPTH * 128, 122 * 64])
    out_d = nc.dram_tensor("out", [nseq * SEQ, D], F32, kind="ExternalOutput").ap()

    b_f1i = dscr("b_f1i", [DEPTH * D, 2 * DFF], BF16); b_f1o = dscr("b_f1o", [DEPTH * DFF, D], BF16)
    b_f2i = dscr("b_f2i", [DEPTH * D, 2 * DFF], BF16); b_f2o = dscr("b_f2o", [DEPTH * DFF, D], BF16)
    b_inp = dscr("b_inp", [DEPTH * D, 3840], BF16); b_gat = dscr("b_gat", [DEPTH * 3 * D, D], BF16)
    b_brn = dscr("b_brn", [DEPTH * 3 * 512, D], BF16); b_out = dscr("b_out", [DEPTH * D, D], BF16)
    Hs = dscr("Hs", [NTOK, D], F32)
    QKs = dscr("QKs", [2 * nseq * NQK * 128, L], BF16)
    Vs = dscr("Vs", [2 * nseq * L, NVC], BF16)

    import contextlib
    es = contextlib.ExitStack()

    def sb(name, shape, dt):
        return es.enter_context(nc.sbuf_tensor("sb_" + name, list(shape), dt))

    NTAB = 96
    NSLOT = dbg.get("nslot", 4)
    with es:
        h_tm = sb("h_tm", [128, 5, D], F32)
        junk = sb("junk", [128, 512], BF16)
        stat = sb("stat", [128, 64], F32)
        xhat = sb("xhat", [128, 2, D], BF16)
        xnT = sb("xnT", [128, 8, 528], BF16)
        big = sb("big", [128, 2 * 8448], BF16)
        wring = sb("wring", [128, NSLOT, 4096], BF16)
        sg = sb("sg", [128, 2, 512], F32)
        qTa = sb("qTa", [128, 4, 528], BF16); qTb = sb("qTb", [128, 4, 528], BF16); qTc = sb("qTc", [128, 4, 528], BF16)
        kTb = sb("kTb", [128, 4, 16 + 1024], BF16); Vb = sb("Vb", [128, 9, 520], BF16)
        kTc = sb("kTc", [128, 2, 16 + 768], BF16); Vc = sb("Vc", [128, 7, 130], BF16)
        ptb = sb("ptb", [128, 3, 512], BF16)
        y_tm = sb("y_tm", [128, 5, 512], BF16)
        yT = sb("yT", [128, 12, 528], BF16)
        vst = sb("vst", [128, 2, NVC], BF16)
        ident = sb("ident", [128, 128], BF16)
        sel = sb("sel", [128, 2, 128], BF16)
        atoms = sb("atoms", [128, 122, 64], BF16)
        zt = sb("zt", [128, 128], BF16)
        gcols = sb("gcols", [128, 3 * DEPTH, 8], F32)
        lamb = sb("lamb", [128, 256], F32)
        btab = sb("btab", [128, NTAB, 128], BF16)
        lamt = sb("lamt", [128, 64], F32)
        lamc = sb("lamc", [128, 32], F32)
        subg = sb("subg", [128, DEPTH, 128], F32)
        sinkb = sb("sinkb", [128, DEPTH * 8], F32)
        t5b = sb("t5b", [128, 384], F32)

        ps = [es.enter_context(nc.psum_tensor(f"ps{i}", [128, 512], F32)) for i in range(6)]
        pst = [es.enter_context(nc.psum_tensor(f"pst{i}", [128, 1024], BF16)) for i in range(2)]
        sems = {e: es.enter_context(nc.semaphore(f"sem_{e}")) for e in Sched.ENGS}
        lanes = {"sp": [es.enter_context(nc.semaphore(f"lsp{i}")) for i in range(dbg.get("splanes", 16))],
                 "pool": [es.enter_context(nc.semaphore(f"lpl{i}")) for i in range(20)],
                 "cast": [es.enter_context(nc.semaphore(f"lca{i}")) for i in range(dbg.get("castlanes", 3))]}
        block = es.enter_context(nc.Block())

        hT = big[:, 0:22 * 528].rearrange("p (k n) -> p k n", k=22)
        mrg_acc = big[:, 0:4 * 528 * 2].bitcast(F32).rearrange("p (k n) -> p k n", k=4)
        mrgT = big[:, 4 * 528 * 2: 4 * 528 * 2 + 8 * 528].rearrange("p (k n) -> p k n", k=8)
        kva_k = [big[:, i * 8448: i * 8448 + L] for i in range(2)]
        kva_v = [big[:, i * 8448 + 4128: i * 8448 + 4128 + 33 * 129].rearrange("p (t c) -> p t c", c=129)
                 for i in range(2)]
        BIG = ["big", "kva0", "kva1"]
        qkst = yT[:, 0:8, :].rearrange("p (a c) n -> p a c n", a=2)
        YTA = ["yT0", "yT1", "yT2"]
        o0n = sg[:, 0, :].rearrange("p (a c) -> p a c", a=4)
        dA = sg[:, 1, :].rearrange("p (a c) -> p a c", a=4)
        gfin = vst[:, :, :].rearrange("p a n -> p (a n)")[:, 0:2 * D].bitcast(F32)

        def mm(out, lhsT, rhs, start, stop, R, W):
            S.add("pe", lambda e: e.matmul(out, lhsT, rhs, start=start, stop=stop, skip_group_check=True), R, W)

        def tr(out, in_, idn, R, W):
            S.add("pe", lambda e: e.transpose(out, in_, idn), R, W)

        def act(out, in_, func, R, W, bias=None, scale=None, accum=None):
            kw = {}
            if bias is not None: kw["bias"] = bias
            if scale is not None: kw["scale"] = scale
            if accum is not None: kw["accum_out"] = accum
            S.add("act", lambda e: e.activation(out=out, in_=in_, func=func, **kw), R, W)

        def dma(out, in_, R, W, q="sp"):
            S.add(q, lambda e: e.dma_start(out=out, in_=in_), R, W, dma=True)

        def dma_nc(out, in_, R, W, q="sp"):
            S.add(q, lambda e: e.dma_start(out=out, in_=in_, allow_slow_non_contiguous=True), R, W, dma=True)

        def vts(out, in0, s1, s2, op0, op1, R, W):
            if s2 is None:
                S.add("dve", lambda e: e.tensor_scalar(out=out, in0=in0, scalar1=s1, scalar2=None, op0=op0), R, W)
            else:
                S.add("dve", lambda e: e.tensor_scalar(out=out, in0=in0, scalar1=s1, scalar2=s2, op0=op0, op1=op1), R, W)

        def vtt(out, in0, in1, op, R, W):
            S.add("dve", lambda e: e.tensor_tensor(out=out, in0=in0, in1=in1, op=op), R, W)

        def vstt(out, in0, sc, in1, op0, op1, R, W):
            S.add("dve", lambda e: e.scalar_tensor_tensor(out=out, in0=in0, scalar=sc, in1=in1, op0=op0, op1=op1), R, W)

        def vrec(out, in_, R, W):
            S.add("dve", lambda e: e.reciprocal(out, in_), R, W)

        def vcopy(out, in_, R, W):
            S.add("dve", lambda e: e.tensor_copy(out, in_), R, W)

        def vmemset(ap, val, R, W):
            S.add("dve", lambda e: e.memset(ap, val), R, W)

        dma(ident[:], ident_in[:, :], [], ["ident"], q="cast")
        dma(sel[:].rearrange("p a m -> p (a m)"), sel_in[:, :], [], ["sel"], q="cast")
        dma(gcols[:].rearrange("p a k -> p (a k)"), gcols_in[:, :], [], ["gcols"])
        if "bcast" not in dbg.get("skip", ()):
            dma(subg[:].rearrange("p a k -> p (a k)"), subg_in.partition_broadcast(128), [], ["subg"])
            dma(sinkb[:], sink_in.partition_broadcast(128), [], ["sinkb"])
            dma(t5b[:], t5_in.partition_broadcast(128), [], ["t5b"])
        vmemset(vst[:], 1.0, [], ["vst0", "vst1"])
        vmemset(zt[:], 0.0, [], ["zt"])
        act(sinkb[:], sinkb[:], AF.Exp, ["sinkb"], ["sinkb"])

        WT = {}

        def cast_jobs(src, dst, rows, row0, name, step=64, relayout=False):
            jobs = []
            for r in range(0, rows, step):
                rr = min(step, rows - r)
                o = dst[row0 + r: row0 + r + rr, :]
                i = src[row0 + r: row0 + r + rr, :]
                if relayout:
                    for tt in range(2):
                        o2 = o.rearrange("r (j t c) -> r j t c", j=11, t=2, c=256)[:, :, tt, :]
                        i2 = i[:, tt * DFF:(tt + 1) * DFF].rearrange("r (j c) -> r j c", c=256)
                        jobs.append((o2, i2, f"{name}_{r}_{tt}"))
                        WT.setdefault(name, []).append(f"{name}_{r}_{tt}")
                    continue
                jobs.append((o, i, f"{name}_{r}"))
                WT.setdefault(name, []).append(f"{name}_{r}")
            return jobs

        def jobs_a(l):
            return (cast_jobs(w_f1i, b_f1i, D, l * D, f"Wf1i{l}", step=32, relayout=True) + cast_jobs(w_f1o, b_f1o, DFF, l * DFF, f"Wf1o{l}")
                    + cast_jobs(w_inp, b_inp, D, l * D, f"Winp{l}"))

        def jobs_b(l):
            return (cast_jobs(w_gat, b_gat, 3 * D, l * 3 * D, f"Wgat{l}") + cast_jobs(w_brn, b_brn, 3 * 512, l * 3 * 512, f"Wbrn{l}")
                    + cast_jobs(w_out, b_out, D, l * D, f"Wout{l}")
                    + cast_jobs(w_f2i, b_f2i, D, l * D, f"Wf2i{l}", step=32, relayout=True) + cast_jobs(w_f2o, b_f2o, DFF, l * DFF, f"Wf2o{l}"))

        def issue_casts(jobs):
            if 'casts' in dbg.get('skip', ()):
                return
            for (o, i, name) in jobs:
                dma(o, i, [], [name], q="cast")

        wr_state = {"i": 0}

        def wslot():
            i = wr_state["i"] % NSLOT
            wr_state["i"] += 1
            return i

        def wload_cols(bw, row0, nk, c0, ncols, name):
            s = wslot()
            v = wring[:, s, 0:nk * ncols].rearrange("p (k c) -> p k c", k=nk)
            dma(v, bw[row0: row0 + nk * 128, c0: c0 + ncols].rearrange("(k p) c -> p k c", p=128), WT[name], [f"w{s}"])
            return s, v

        def hb(bi):
            return f"h{bi}"

        def norm_stage(t, gidx):
            nb = len(t.blocks)
            for bi, (off, bs) in enumerate(t.blocks):
                act(xhat[:bs, bi % 2, :], h_tm[:bs, bi, :], AF.Square, [hb(bi)], [f"ss{bi}", f"xhat{bi % 2}"], accum=stat[:bs, bi:bi + 1])
            for bi, (off, bs) in enumerate(t.blocks):
                act(stat[:bs, 8 + bi:9 + bi], stat[:bs, bi:bi + 1], AF.Sqrt, [f"ss{bi}"], [f"sd{bi}"], scale=1.0 / D, bias=EPS)
            for bi, (off, bs) in enumerate(t.blocks):
                vrec(stat[:bs, 16 + bi:17 + bi], stat[:bs, 8 + bi:9 + bi], [f"sd{bi}"], [f"rs{bi}"])
            for bi, (off, bs) in enumerate(t.blocks):
                xi = bi % 2
                vts(xhat[:bs, xi, :], h_tm[:bs, bi, :], stat[:bs, 16 + bi:17 + bi], None, ALU.mult, None,
                    [hb(bi), f"rs{bi}"], [f"xhat{xi}"])
                pv = pst[xi][:, :].rearrange("p (k n) -> p k n", k=8)
                for k in range(8):
                    tr(pv[:, k, :bs], xhat[:bs, xi, k * 128:(k + 1) * 128], ident[:bs, :bs],
                       [f"xhat{xi}", "ident"], [f"pst{xi}"])
                vtt(xnT[:, :, off:off + bs], pv[:, :, :bs], gcols[:, gidx, :].unsqueeze(2).to_broadcast([128, 8, bs]),
                    ALU.mult, [f"pst{xi}", "gcols"], [f"xn{bi}"])

        def xn_tokens(t):
            return [f"xn{bi}" for bi in range(len(t.blocks))]

        def ffn_stage(t, l, which):
            bi_w, bo_w = (b_f1i, b_f1o) if which == 1 else (b_f2i, b_f2o)
            ni, no = (f"Wf1i{l}", f"Wf1o{l}") if which == 1 else (f"Wf2i{l}", f"Wf2o{l}")
            norm_stage(t, l * 3 + (0 if which == 1 else 2))
            XN = xn_tokens(t)
            cnt = 0
            for j in range(11):
                s = wslot()
                v = wring[:, s, :].rearrange("p (k c) -> p k c", k=8)
                dma(v, bi_w[l * D:(l + 1) * D, j * 512:(j + 1) * 512].rearrange("(k p) c -> p k c", p=128),
                    WT[ni], [f"w{s}"])
                for c in range(2):
                    fc = j * 2 + c
                    for (off, cs) in t.chunks:
                        pg = ps[(cnt % 2) * 2]; pu = ps[(cnt % 2) * 2 + 1]
                        tg = f"ps{(cnt % 2) * 2}"; tu = f"ps{(cnt % 2) * 2 + 1}"
                        si = cnt % 2
                        cnt += 1
                        for k in range(8):
                            mm(pg[:, :cs], v[:, k, c * 128:(c + 1) * 128], xnT[:, k, off:off + cs], k == 0, k == 7,
                               [f"w{s}"] + XN, [tg])
                        for k in range(8):
                            mm(pu[:, :cs], v[:, k, 256 + c * 128:256 + (c + 1) * 128], xnT[:, k, off:off + cs], k == 0, k == 7,
                               [f"w{s}"] + XN, [tu])
                        act(sg[:, si, :cs], pg[:, :cs], AF.Silu, [tg], [f"sg{si}"])
                        vtt(hT[:, fc, off:off + cs], sg[:, si, :cs], pu[:, :cs], ALU.mult, [f"sg{si}", tu], BIG)
            nb = len(t.blocks)
            for half in range(2):
                banks = [1 + bi for bi in range(nb)]
                for (k0, nk) in ((0, 8), (8, 8), (16, 6)):
                    s, v = wload_cols(bo_w, l * DFF + k0 * 128, nk, half * 512, 512, no)
                    for bi, (off, bs) in enumerate(t.blocks):
                        for kk in range(nk):
                            k = k0 + kk
                            mm(ps[banks[bi]][:bs, :], hT[:, k, off:off + bs], v[:, kk, :], k == 0, k == 21,
                               [f"w{s}"] + BIG, [f"ps{banks[bi]}"])
                for bi, (off, bs) in enumerate(t.blocks):
                    vstt(h_tm[:bs, bi, half * 512:(half + 1) * 512], ps[banks[bi]][:bs, :], 0.5,
                         h_tm[:bs, bi, half * 512:(half + 1) * 512], ALU.mult, ALU.add, [f"ps{banks[bi]}", hb(bi)], [hb(bi)])

        def proj_stage(t, l, par):
            norm_stage(t, l * 3 + 1)
            XN = xn_tokens(t)
            nm = f"Winp{l}"
            qkbase = (par * nseq + t.seq) * NQK * 128
            fm = [(0, 4, 0.125, 0), (512, 4, None, 4), (1536, 4, 0.125, 8), (2048, 4, None, 12), (3072, 4, 0.125, 16)]
            cnt = 0
            for gi, (c0, nch, scl, ch0) in enumerate(fm + [(None, 2, None, 20)]):
                si = gi % 2
                if c0 is not None:
                    s, v = wload_cols(b_inp, l * D, 8, c0, 512, nm)
                else:
                    s = wslot()
                    v = wring[:, s, :].rearrange("p (k c) -> p k c", k=8)
                    src = b_inp[l * D:(l + 1) * D, :].rearrange("(k p) c -> p k c", p=128)
                    dma_nc(v[:, :, 0:128], src[:, :, 3584:3712], WT[nm], [f"w{s}"])
                    dma_nc(v[:, :, 128:192], src[:, :, 3648:3712], WT[nm], [f"w{s}"])
                    dma_nc(v[:, :, 192:256], src[:, :, 3584:3648], WT[nm], [f"w{s}"])
                    nch, scl, ch0 = 2, None, 20
                for c in range(nch):
                    for (off, cs) in t.chunks:
                        b = cnt % 2; cnt += 1
                        for k in range(8):
                            mm(ps[b][:, :cs], v[:, k, c * 128:(c + 1) * 128], xnT[:, k, off:off + cs], k == 0, k == 7,
                               [f"w{s}"] + XN, [f"ps{b}"])
                        if scl is not None:
                            act(qkst[:, si, c, off:off + cs], ps[b][:, :cs], AF.Copy, [f"ps{b}"], [f"qkst{si}"] + YTA, scale=scl)
                        else:
                            vcopy(qkst[:, si, c, off:off + cs], ps[b][:, :cs], [f"ps{b}"], [f"qkst{si}"] + YTA)
                dst = QKs[qkbase + ch0 * 128: qkbase + (ch0 + nch) * 128, t.pos0:t.pos0 + t.n].rearrange("(c p) n -> p c n", p=128)
                dma(dst, qkst[:, si, 0:nch, 0:t.n], [f"qkst{si}"], [f"QK{par}_{t.seq}_{t.idx}"], q="sp")
            vsl = []
            for (c0, ncols) in ((1024, 512), (2560, 512)):
                vsl.append(wload_cols(b_inp, l * D, 8, c0, ncols, nm))
            s3, v3 = wload_cols(b_inp, l * D, 8, 3712, 128, nm)
            vbase = (par * nseq + t.seq) * L
            for bi, (off, bs) in enumerate(t.blocks):
                vi = bi % 2
                for gi2 in range(3):
                    b = 2 + (bi * 3 + gi2) % 4
                    if gi2 < 2:
                        s, v = vsl[gi2]; ncols = 512
                    else:
                        s, v = s3, v3; ncols = 128
                    for k in range(8):
                        mm(ps[b][:bs, :ncols], xnT[:, k, off:off + bs], v[:, k, :ncols], k == 0, k == 7,
                           [f"w{s}"] + XN, [f"ps{b}"])
                    if gi2 == 0:
                        dv = vst[:bs, vi, VA0:VA0 + 516].rearrange("p (h c) -> p h c", c=129)[:, :, 0:128]
                        sv = ps[b][:bs, :512].rearrange("p (h c) -> p h c", c=128)
                    elif gi2 == 1:
                        dv = vst[:bs, vi, VB0:VB0 + 520].rearrange("p (h c) -> p h c", c=65)[:, :, 0:64]
                        sv = ps[b][:bs, :512].rearrange("p (h c) -> p h c", c=64)
                    else:
                        dv = vst[:bs, vi, VC0:VC0 + 130].rearrange("p (h c) -> p h c", c=65)[:, :, 0:64]
                        sv = ps[b][:bs, :128].rearrange("p (h c) -> p h c", c=64)
                    if gi2 == 1:
                        S.add("act", (lambda dv=dv, sv=sv: (lambda e: e.copy(dv, sv)))(), [f"ps{b}"], [f"vst{vi}"])
                    else:
                        vcopy(dv, sv, [f"ps{b}"], [f"vst{vi}"])
                dma(Vs[vbase + t.pos0 + off: vbase + t.pos0 + off + bs, :], vst[:bs, vi, :], [f"vst{vi}"],
                    [f"V{par}_{t.seq}_{t.idx}"], q="sp")

        def seq_tokens(kind, par, seq):
            return [f"{kind}{par}_{seq}_{i}" for i in range(8)]

        def qpos_of(t, bi):
            return t.pos0 + t.blocks[bi][0]

        st_state = {"i": 0}

        def next_st():
            i = st_state["i"] % 2
            st_state["i"] += 1
            p = st_state["i"] % 3
            return i, p

        def transpose_y(t, branch):
            if dbg.get("dump_y") == branch:
                for bi, (off, bs) in enumerate(t.blocks):
                    vts(h_tm[:bs, bi, 0:512], y_tm[:bs, bi, :], 1.0, None, ALU.mult, None, [f"ytm{bi}", hb(bi)], [hb(bi)])
            for bi, (off, bs) in enumerate(t.blocks):
                xi = bi % 2
                pv = pst[xi][:, 0:512].rearrange("p (k n) -> p k n", k=4)
                for c in range(4):
                    tr(pv[:, c, :bs], y_tm[:bs, bi, c * 128:(c + 1) * 128], ident[:bs, :bs], [f"ytm{bi}", "ident"], [f"pst{xi}"])
                vts(yT[:, branch * 4:(branch + 1) * 4, off:off + bs], pv[:, :, :bs], 1.0, None, ALU.mult, None, [f"pst{xi}"], [f"yT{branch}", "qkst0", "qkst1"])

        def attn_A(t, l, par):
            qkbase = (par * nseq + t.seq) * NQK * 128
            vbase = (par * nseq + t.seq) * L
            QKT = seq_tokens("QK", par, t.seq); VT = seq_tokens("V", par, t.seq)
            lcol = lamc[:, l: l + 1]
            for h in range(4):
                ri = h % 2
                kk = kva_k[ri]; vv = kva_v[ri]
                r0 = qkbase + (4 + h) * 128
                KV = [f"kva{ri}"]
                dma(kk, QKs[r0:r0 + 128, :], QKT, KV, q="sp")
                dma_nc(vv[:16, 0, :], Vs[vbase: vbase + 16, VA0 + h * 129: VA0 + (h + 1) * 129], VT, KV, q="sp")
                for t4 in range(8):
                    dma_nc(vv[:, 1 + 4 * t4: 5 + 4 * t4, :],
                           Vs[vbase + 16 + 512 * t4: vbase + 16 + 512 * (t4 + 1), VA0 + h * 129: VA0 + (h + 1) * 129].rearrange("(t p) c -> p t c", p=128),
                           VT, KV, q="sp")
                for grp in (t.groups if dbg.get('a_mode', 2) >= 1 else []):
                    segs = [(bi, t.blocks[bi][0], t.blocks[bi][1]) for bi in grp]
                    g0 = segs[0][1]; gn = sum(sg_[2] for sg_ in segs)
                    order = sorted(range(len(segs)), key=lambda i: -segs[i][2])
                    for m in range(2):
                        ob = [2 + 2 * m, 3 + 2 * m]
                        first = {ob[0]: True, ob[1]: True}
                        pend = None
                        for kt in range(34):
                            cur = None
                            if kt < 33:
                                nk = 16 if kt == 0 else 128
                                kp0 = 0 if kt == 0 else 16 + 128 * (kt - 1)
                                sti, pti = next_st()
                                stb = ps[sti]
                                lo = kp0 - (t.pos0 + g0 + gn - 1); hi = (kp0 + nk - 1) - (t.pos0 + g0)
                                far = "P" if lo >= 91 else ("N" if hi <= -91 else None)
                                mm(stb[:nk, :gn], kk[m * 64:(m + 1) * 64, kp0:kp0 + nk], qTa[m * 64:(m + 1) * 64, h, g0:g0 + gn],
                                   True, far is not None, KV + ["qTa"], [f"ps{sti}"])
                                if far is None:
                                    for si_, (bi, off, bs) in enumerate(segs):
                                        ti = tab.get("A", h, kp0, nk, t.pos0 + off, bs)
                                        mm(stb[:nk, off - g0: off - g0 + bs], ident[:, :nk], btab[:, ti, :bs], False,
                                           si_ == len(segs) - 1, ["ident", "btab"], [f"ps{sti}"])
                                    act(ptb[:nk, pti, :gn], stb[:nk, :gn], AF.Exp, [f"ps{sti}"], [f"pt{pti}"])
                                else:
                                    col = (31 if far == "P" else 15) * 12 + h
                                    act(ptb[:nk, pti, :gn], stb[:nk, :gn], AF.Exp, [f"ps{sti}", "t5b"], [f"pt{pti}"],
                                        bias=t5b[:nk, col:col + 1])
                                cur = (kt, nk, pti)
                            if pend is not None and dbg.get('a_mode', 2) >= 1.5:
                                pkt, pnk, ppt = pend
                                for oi in order:
                                    bi, off, bs = segs[oi]
                                    bank = ob[oi // 2]; c0 = (oi % 2) * 129
                                    mm(ps[bank][:bs, c0:c0 + 129], ptb[:pnk, ppt, off - g0: off - g0 + bs], vv[:pnk, pkt, :],
                                       first[bank], pkt == 32, [f"pt{ppt}"] + KV, [f"ps{bank}"])
                                    first[bank] = False
                            pend = cur
                        if dbg.get('a_mode', 2) < 1.6:
                            continue
                        for oi, (bi, off, bs) in enumerate(segs):
                            bank = ob[oi // 2]; c0 = (oi % 2) * 129
                            vrec(stat[:bs, 24 + m * 4 + oi: 25 + m * 4 + oi], ps[bank][:bs, c0 + 128:c0 + 129], [f"ps{bank}"], [f"rz{m}_{oi}"])
                        if m == 0:
                            for oi, (bi, off, bs) in enumerate(segs):
                                bank = ob[oi // 2]; c0 = (oi % 2) * 129
                                vts(o0n[:bs, oi, :], ps[bank][:bs, c0:c0 + 128], stat[:bs, 24 + oi:25 + oi], None, ALU.mult, None,
                                    [f"ps{bank}", f"rz0_{oi}"], [f"o0n{oi}", "sg0"])
                        else:
                            for oi, (bi, off, bs) in enumerate(segs):
                                vts(stat[:bs, 32 + oi:33 + oi], stat[:bs, 28 + oi:29 + oi], lcol[:bs, :], None, ALU.mult, None,
                                    [f"rz1_{oi}", "lamc"], [f"c1_{oi}"])
                            for oi, (bi, off, bs) in enumerate(segs):
                                bank = ob[oi // 2]; c0 = (oi % 2) * 129
                                vstt(dA[:bs, oi, :], ps[bank][:bs, c0:c0 + 128], stat[:bs, 32 + oi:33 + oi], o0n[:bs, oi, :],
                                     ALU.mult, ALU.add, [f"ps{bank}", f"c1_{oi}", f"o0n{oi}", "sg0"], [f"dA{oi}", "sg1"])
                    if dbg.get('a_mode', 2) < 1.8:
                        continue
                    for oi, (bi, off, bs) in enumerate(segs):
                        act(junk[:bs, oi * 128:(oi + 1) * 128], dA[:bs, oi, :], AF.Square, [f"dA{oi}", "sg1"], [f"ssA{oi}", f"junk{oi}"], accum=stat[:bs, 36 + oi:37 + oi])
                    for oi, (bi, off, bs) in enumerate(segs):
                        act(stat[:bs, 40 + oi:41 + oi], stat[:bs, 36 + oi:37 + oi], AF.Sqrt, [f"ssA{oi}"], [f"sdA{oi}"], scale=1.0 / 128, bias=EPS)
                    for oi, (bi, off, bs) in enumerate(segs):
                        vrec(stat[:bs, 44 + oi:45 + oi], stat[:bs, 40 + oi:41 + oi], [f"sdA{oi}"], [f"rsA{oi}"])
                    for oi, (bi, off, bs) in enumerate(segs):
                        vstt(y_tm[:bs, bi, h * 128:(h + 1) * 128], dA[:bs, oi, :], stat[:bs, 44 + oi:45 + oi], subg[:bs, l, :],
                             ALU.mult, ALU.mult, [f"dA{oi}", "sg1", f"rsA{oi}", "subg"], [f"ytm{bi}"])
            if dbg.get('a_mode', 2) >= 2:
                transpose_y(t, 0)

        def attn_BC(t, l, par, mixer, loads_only=False):
            qkbase = (par * nseq + t.seq) * NQK * 128
            vbase = (par * nseq + t.seq) * L
            QKT = seq_tokens("QK", par, t.seq); VT = seq_tokens("V", par, t.seq)
            if mixer == "B":
                r0 = 8 * t.idx
                rlo = max(0, r0 - 4); rhi = min(64, r0 + 12)
                nrow = rhi - rlo
                qT = qTb; kT = kTb; Vt = Vb; KR = ["kTb", "qTb"]; VR = ["Vb"]
            else:
                j0 = t.qb[1] if t.idx == 0 else t.qb[0]
                jlo = max(0, j0 - 1); jhi = min(32, j0 + 5)
                nblk = jhi - jlo
                qT = qTc; kT = kTc; Vt = Vc; KR = ["kTc", "qTc"]; VR = ["Vc"]
            if loads_only and mixer == "B":
                src = QKs[qkbase + 12 * 128: qkbase + 16 * 128, :].rearrange("(c p) n -> p c n", p=128)
                dma(kTb[:, :, 0:16], src[:, :, 0:16], QKT, ["kTb"], q="sp")
                for c4 in range(4):
                    dma(kTb[:, c4, 16:16 + nrow * 64], src[:, c4, 16 + rlo * 64: 16 + rhi * 64], QKT, ["kTb"], q="sp")
                dma(Vb[:16, 0, :], Vs[vbase: vbase + 16, VB0:VB0 + 520], VT, ["Vb"], q="sp")
                for t2 in range(0, nrow // 2, 2):
                    dma(Vb[:, 1 + t2: 3 + t2, :], Vs[vbase + 16 + rlo * 64 + t2 * 128: vbase + 16 + rlo * 64 + (t2 + 2) * 128, VB0:VB0 + 520].rearrange("(t p) c -> p t c", p=128),
                        VT, ["Vb"], q="sp")
            if loads_only and mixer == "C":
                src = QKs[qkbase + 20 * 128: qkbase + 22 * 128, :].rearrange("(c p) n -> p c n", p=128)
                dma(kTc[:, :, 0:16], src[:, :, 0:16], QKT, ["kTc"], q="sp")
                dma(kTc[:, :, 16:16 + nblk * 128], src[:, :, 16 + jlo * 128: 16 + jhi * 128], QKT, ["kTc"], q="sp")
                dma(Vc[:16, 0, :], Vs[vbase: vbase + 16, VC0:VC0 + 130], VT, ["Vc"], q="sp")
                for t2 in range(0, nblk, 2):
                    n2 = min(2, nblk - t2)
                    dma_nc(Vc[:, 1 + t2: 1 + t2 + n2, :], Vs[vbase + 16 + (jlo + t2) * 128: vbase + 16 + (jlo + t2 + n2) * 128, VC0:VC0 + 130].rearrange("(t p) c -> p t c", p=128),
                           VT, ["Vc"], q="sp")
            if loads_only:
                return
            job = 0
            for bi, (off, bs) in enumerate(t.blocks if dbg.get("b_mode", 2) >= 1 else []):
                jb = t.qb[bi]
                if "b_blocks" in dbg and bi not in dbg["b_blocks"]:
                    continue
                kts = [(16, 0, 0, ("meta",))]
                if mixer == "B":
                    if jb < 0:
                        rows = [(0, "mq"), (0, "mq")]
                        tl = range(0, 4)
                    else:
                        r = 2 * jb
                        rs0 = min(max(r - 4, 0), 56); rs1 = min(max(r + 1 - 4, 0), 56)
                        tl = range(rs0 // 2, (rs1 + 7) // 2 + 1)
                    for tt in tl:
                        kts.append((128, 16 + (2 * tt - rlo) * 64, 1 + (2 * tt - rlo) // 2, ("rows", tt)))
                else:
                    if jb < 0:
                        bl = [0]
                    else:
                        bl = [j for j in (jb - 1, jb, jb + 1) if 0 <= j < 32]
                    for j in bl:
                        kts.append((128, 16 + (j - jlo) * 128, 1 + (j - jlo), ("blk", j)))
                if "b_maxkt" in dbg:
                    kts = kts[:dbg["b_maxkt"]]
                if "b_kts" in dbg:
                    kts = [kts[i] for i in dbg["b_kts"] if i < len(kts)]
                for hg in range(dbg.get("b_nhg", 2)):
                    ob = 2 + (job % 4); job += 1
                    first = True
                    pend = None
                    for ki in range(len(kts) + 1):
                        cur = None
                        if ki < len(kts):
                            nk, kc0, vti, kinfo = kts[ki]
                            sti, pti = next_st()
                            stb = ps[sti]
                            firstst = True
                            for s_ in range(4):
                                hd = hg * 4 + s_
                                ch = hd // 2; half = hd % 2
                                if mixer == "B":
                                    ksel = kT[half * 64:(half + 1) * 64, ch, kc0:kc0 + nk]
                                else:
                                    kv = hg
                                    ksel = kT[half * 64:(half + 1) * 64, 0 if half == kv else 1, kc0:kc0 + nk]
                                nbias = 0
                                blist = []
                                if mixer == "B" and kinfo[0] == "meta":
                                    blist.append((ident[:, :nk], zt[:, :bs], s_ * 128, bs))
                                if mixer == "B" and kinfo[0] == "rows":
                                    tt = kinfo[1]
                                    if jb < 0:
                                        for a in range(2):
                                            blist.append((sel[:, a, :], atoms[:, 121, 0:16], s_ * 128, 16))
                                    else:
                                        r = 2 * jb
                                        for a in range(2):
                                            for b_ in range(2):
                                                rq = r + b_; kr = 2 * tt + a
                                                rs = min(max(rq - 4, 0), 56)
                                                vis = rs <= kr < rs + 8
                                                ai = hd * 15 + (kr - rq + 7) if vis else 120
                                                blist.append((sel[:, a, :], atoms[:, ai, :], s_ * 128 + b_ * 64, 64))
                                if mixer == "C" and not dbg.get("c_nobias"):
                                    kp0 = 0 if kinfo[0] == "meta" else 16 + 128 * kinfo[1]
                                    ti = tab.get("C", hd, kp0, nk, t.pos0 + off, bs)
                                    blist.append((ident[:, :nk], btab[:, ti, :bs], s_ * 128, bs))
                                mm(stb[:nk, s_ * 128: s_ * 128 + bs], ksel, qT[half * 64:(half + 1) * 64, ch, off:off + bs],
                                   firstst, len(blist) == 0 and s_ == 3, KR, [f"ps{sti}"])
                                firstst = False
                                for bj, (lh, rh, cc0, cn) in enumerate(blist):
                                    mm(stb[:nk, cc0:cc0 + cn], lh, rh, False, (s_ == 3 and bj == len(blist) - 1),
                                       ["sel", "atoms", "ident", "btab", "zt"], [f"ps{sti}"])
                            if bs == 128:
                                act(ptb[:nk, pti, :], stb[:nk, :], AF.Exp, [f"ps{sti}"], [f"pt{pti}"])
                            else:
                                act(ptb[:nk, pti, :].rearrange("p (s c) -> p s c", c=128)[:, :, :bs],
                                    stb[:nk, :].rearrange("p (s c) -> p s c", c=128)[:, :, :bs], AF.Exp, [f"ps{sti}"], [f"pt{pti}"])
                            cur = (nk, vti, pti)
                        if pend is not None and dbg.get("b_mode", 2) >= 1.5:
                            pnk, pvti, ppt = pend
                            for s_ in range(4):
                                hd = hg * 4 + s_
                                vcol = hd * 65 if mixer == "B" else hg * 65
                                mm(ps[ob][:bs, s_ * 65:(s_ + 1) * 65], ptb[:pnk, ppt, s_ * 128: s_ * 128 + bs], Vt[:pnk, pvti, vcol:vcol + 65],
                                   first, ki == len(kts) and s_ == 3, [f"pt{ppt}"] + VR, [f"ps{ob}"])
                                first = False
                        pend = cur
                    if dbg.get("b_mode", 2) < 2:
                        continue
                    ov = ps[ob][:bs, 0:260].rearrange("p (s c) -> p s c", c=65)
                    zc = 48 + hg * 4
                    if mixer == "C":
                        vtt(stat[:bs, zc:zc + 4], ov[:, :, 64], sinkb[:bs, l * 8 + hg * 4: l * 8 + hg * 4 + 4], ALU.add,
                            [f"ps{ob}", "sinkb"], [f"zc{hg}"])
                        vrec(stat[:bs, zc + 8:zc + 12], stat[:bs, zc:zc + 4], [f"zc{hg}"], [f"rzb{hg}"])
                    else:
                        vts(stat[:bs, zc:zc + 4], ov[:, :, 64], 1.0, None, ALU.mult, None, [f"ps{ob}"], [f"zc{hg}"])
                        vrec(stat[:bs, zc + 8:zc + 12], stat[:bs, zc:zc + 4], [f"zc{hg}"], [f"rzb{hg}"])
                    vtt(y_tm[:bs, bi, hg * 256:(hg + 1) * 256].rearrange("p (s c) -> p s c", c=64), ov[:, :, 0:64],
                        stat[:bs, zc + 8:zc + 12].unsqueeze(2).to_broadcast([bs, 4, 64]), ALU.mult, [f"ps{ob}", f"rzb{hg}"], [f"ytm{bi}"])
            if dbg.get("b_mode", 2) >= 2:
                transpose_y(t, 1 if mixer == "B" else 2)

        def load_q(t, par):
            qkbase = (par * nseq + t.seq) * NQK * 128
            for (buf, ch0, nm) in ((qTa, 0, "qTa"), (qTb, 8, "qTb"), (qTc, 16, "qTc")):
                src = QKs[qkbase + ch0 * 128: qkbase + (ch0 + 4) * 128, t.pos0:t.pos0 + t.n].rearrange("(c p) n -> p c n", p=128)
                dma(buf[:, :, 0:t.n], src, [f"QK{par}_{t.seq}_{t.idx}"], [nm], q="sp")

        def merge_stage(t, l):
            norm_stage(t, l * 3 + 1)
            XN = xn_tokens(t)
            cnt = 0
            for ch in range(2):
                for i in range(3):
                    sgt, vg = wload_cols(b_gat, (l * 3 + i) * D, 8, ch * 512, 512, f"Wgat{l}")
                    sbt, vb = wload_cols(b_brn, (l * 3 + i) * 512, 4, ch * 512, 512, f"Wbrn{l}")
                    for c in range(4):
                        for (off, cs) in t.chunks:
                            b = (cnt % 2) * 2; cnt += 1
                            si = (cnt % 2)
                            for k in range(8):
                                mm(ps[b][:, :cs], vg[:, k, c * 128:(c + 1) * 128], xnT[:, k, off:off + cs], k == 0, k == 7,
                                   [f"w{sgt}"] + XN, [f"ps{b}"])
                            for k in range(4):
                                mm(ps[b + 1][:, :cs], vb[:, k, c * 128:(c + 1) * 128], yT[:, i * 4 + k, off:off + cs], k == 0, k == 3,
                                   [f"w{sbt}", f"yT{i}"], [f"ps{b + 1}"])
                            act(sg[:, si, :cs], ps[b][:, :cs], AF.Sigmoid, [f"ps{b}"], [f"sg{si}"])
                            if i == 0:
                                vtt(mrg_acc[:, c, off:off + cs], sg[:, si, :cs], ps[b + 1][:, :cs], ALU.mult, [f"sg{si}", f"ps{b + 1}"], BIG)
                            else:
                                vtt(sg[:, si, :cs], sg[:, si, :cs], ps[b + 1][:, :cs], ALU.mult, [f"sg{si}", f"ps{b + 1}"], [f"sg{si}"])
                                if i == 1:
                                    vtt(mrg_acc[:, c, off:off + cs], mrg_acc[:, c, off:off + cs], sg[:, si, :cs], ALU.add, [f"sg{si}"] + BIG, BIG)
                                else:
                                    vtt(mrgT[:, ch * 4 + c, off:off + cs], mrg_acc[:, c, off:off + cs], sg[:, si, :cs], ALU.add, [f"sg{si}"] + BIG, BIG)
            for half in range(2):
                s, v = wload_cols(b_out, l * D, 8, half * 512, 512, f"Wout{l}")
                for bi, (off, bs) in enumerate(t.blocks):
                    b = 1 + bi
                    for k in range(8):
                        mm(ps[b][:bs, :], mrgT[:, k, off:off + bs], v[:, k, :], k == 0, k == 7, [f"w{s}"] + BIG, [f"ps{b}"])
                    vtt(h_tm[:bs, bi, half * 512:(half + 1) * 512], ps[b][:bs, :], h_tm[:bs, bi, half * 512:(half + 1) * 512], ALU.add,
                        [f"ps{b}", hb(bi)], [hb(bi)])

        def layer_setup(l):
            lam_init = 0.8 - 0.6 * math.exp(-0.3 * l)
            b0 = 0
            dma(lamb[:], lam_in[:, l * 256:(l + 1) * 256].partition_broadcast(128), [], ["lamb"])
            for j in range(2):
                vstt(lamt[:, :], lamb[:, b0 + j * 128: b0 + j * 128 + 64], 1.0, lamb[:, b0 + j * 128 + 64: b0 + j * 128 + 128], ALU.mult, ALU.mult,
                     ["lamb"], ["lamt"])
                S.add("dve", (lambda j=j: (lambda e: e.tensor_reduce(out=lamc[:, 8 + j: 9 + j], in_=lamt[:, :], axis=mybir.AxisListType.X, op=ALU.add)))(),
                      ["lamt"], [f"lams{j}"])
                act(lamc[:, 10 + j: 11 + j], lamc[:, 8 + j: 9 + j], AF.Exp, [f"lams{j}"], [f"lame{j}"])
            vtt(lamc[:, 12:13], lamc[:, 10:11], lamc[:, 11:12], ALU.subtract, ["lame0", "lame1"], ["lamd"])
            vts(lamc[:, l: l + 1], lamc[:, 12:13], -1.0, -lam_init, ALU.mult, ALU.add, ["lamd"], ["lamc"])
            vts(subg[:, l, :], subg[:, l, :], 1.0 - lam_init, None, ALU.mult, None, ["subg"], ["subg"])
            dma(atoms[:].rearrange("p a c -> p (a c)"), atoms_in[l * 128:(l + 1) * 128, :], [], ["atoms"], q="cast")

        ntab_holder = {}
        btab_in = din("btab", [128, NTAB * 128])
        if "btab" not in dbg.get("skip", ()):
            dma(btab[:].rearrange("p a c -> p (a c)"), btab_in[:, :], [], ["btab"], q="cast")

        issue_casts(jobs_a(0))
        npass = dbg.get('npass', depth + 1)
        tiles_run = tiles[:dbg.get('ntiles', len(tiles))]
        for p in range(npass):
            nxt = []
            if p < depth:
                nxt = jobs_b(p) + (jobs_a(p + 1) if p + 1 < depth else [])
            per = (len(nxt) + len(tiles) - 1) // len(tiles) if nxt else 0
            if p >= 1:
                layer_setup(p - 1)
            for ti, t in enumerate(tiles_run if p == 0 else tiles[:dbg.get('ntiles1', len(tiles_run))]):
                issue_casts(nxt[ti * per:(ti + 1) * per])
                nb = len(t.blocks)
                if p == 0:
                    for bi, (off, bs) in enumerate(t.blocks):
                        if t.qb[bi] < 0:
                            dma(h_tm[:bs, bi, :], meta_in[:, :], [], [hb(bi)], q="sp")
                        else:
                            r = t.seq * SEQ + t.qb[bi] * 128
                            dma(h_tm[:bs, bi, :], x_in[r:r + 128, :], [], [hb(bi)], q="sp")
                else:
                    for bi, (off, bs) in enumerate(t.blocks):
                        dma(h_tm[:bs, bi, :], Hs[t.row0 + off: t.row0 + off + bs, :], [f"H{t.seq}_{t.idx}"], [hb(bi)], q="sp")
                if p >= 1:
                    l = p - 1; par = l % 2
                    tl_ = tiles_run if p == 0 else tiles[:dbg.get('ntiles1', len(tiles_run))]
                    if ti == 0:
                        load_q(t, par)
                        attn_BC(t, l, par, "B", loads_only=True)
                        attn_BC(t, l, par, "C", loads_only=True)
                    if 'attnA' in stages: attn_A(t, l, par)
                    if 'attnB' in stages: attn_BC(t, l, par, "B")
                    if 'attnC' in stages: attn_BC(t, l, par, "C")
                    if ti + 1 < len(tl_):
                        tn = tl_[ti + 1]
                        load_q(tn, par)
                        attn_BC(tn, l, par, "B", loads_only=True)
                        attn_BC(tn, l, par, "C", loads_only=True)
                    if 'merge' in stages: merge_stage(t, l)
                    if 'ffn2' in stages: ffn_stage(t, l, 2)
                if p < depth:
                    if 'ffn1' in stages: ffn_stage(t, p, 1)
                    if 'proj' in stages: proj_stage(t, p, p % 2)
                if dbg.get('dump_h') and p == npass - 1:
                    for bi, (off, bs) in enumerate(t.blocks):
                        if t.qb[bi] >= 0:
                            r = t.seq * SEQ + t.qb[bi] * 128
                            dma(out_d[r:r + 128, :], h_tm[:bs, bi, :], [hb(bi)], [f"OUT{r}"], q="sp")
                    continue
                if p < depth:
                    for bi, (off, bs) in enumerate(t.blocks):
                        dma(Hs[t.row0 + off: t.row0 + off + bs, :], h_tm[:bs, bi, :], [hb(bi)], [f"H{t.seq}_{t.idx}"], q="sp")
                else:
                    if ti == 0:
                        dma(gfin, gfin_in.partition_broadcast(128), [], ["gfin", "vst0", "vst1"], q="sp")
                    for bi, (off, bs) in enumerate(t.blocks):
                        if t.qb[bi] < 0:
                            continue
                        act(xhat[:bs, bi % 2, :], h_tm[:bs, bi, :], AF.Square, [hb(bi)], [f"ss{bi}", f"xhat{bi % 2}"], accum=stat[:bs, bi:bi + 1])
                        act(stat[:bs, 8 + bi:9 + bi], stat[:bs, bi:bi + 1], AF.Sqrt, [f"ss{bi}"], [f"sd{bi}"], scale=1.0 / D, bias=EPS)
                        vrec(stat[:bs, 16 + bi:17 + bi], stat[:bs, 8 + bi:9 + bi], [f"sd{bi}"], [f"rs{bi}"])
                        vstt(h_tm[:bs, bi, :], h_tm[:bs, bi, :], stat[:bs, 16 + bi:17 + bi], gfin[:bs, :], ALU.mult, ALU.mult,
                             [hb(bi), f"rs{bi}", "gfin"], [hb(bi)])
                        r = t.seq * SEQ + t.qb[bi] * 128
                        dma(out_d[r:r + 128, :], h_tm[:bs, bi, :], [hb(bi)], [f"OUT{r}"], q="sp")
        assert len(tab.specs) <= NTAB, len(tab.specs)
        S.emit(nc, block, sems, lanes)
    return nc, tab


_CACHE = {}


def _get_program(nseq, depth):
    key = (nseq, depth)
    if key not in _CACHE:
        _CACHE[key] = build_program(nseq, depth)
    return _CACHE[key]


def prep_shared(inp, tab):
    f = np.float32
    sh = {}
    sh["meta"] = np.ascontiguousarray(inp["meta_tokens"], f)
    sh["w_ffn1_in"] = np.ascontiguousarray(inp["w_ffn1_in"], f).reshape(DEPTH * D, 2 * DFF)
    sh["w_ffn1_out"] = np.ascontiguousarray(inp["w_ffn1_out"], f).reshape(DEPTH * DFF, D)
    sh["w_ffn2_in"] = np.ascontiguousarray(inp["w_ffn2_in"], f).reshape(DEPTH * D, 2 * DFF)
    sh["w_ffn2_out"] = np.ascontiguousarray(inp["w_ffn2_out"], f).reshape(DEPTH * DFF, D)
    sh["w_in"] = np.ascontiguousarray(inp["w_in"], f).reshape(DEPTH * D, 3840)
    sh["w_gate"] = np.ascontiguousarray(inp["w_gate"], f).reshape(DEPTH * 3 * D, D)
    sh["w_branch"] = np.ascontiguousarray(inp["w_branch"], f).reshape(DEPTH * 3 * 512, D)
    sh["w_out"] = np.ascontiguousarray(inp["w_out"], f).reshape(DEPTH * D, D)
    g = np.stack([np.asarray(inp["norm_ffn1"], f), np.asarray(inp["norm_mix"], f), np.asarray(inp["norm_ffn2"], f)], axis=1)
    sh["gcols"] = np.ascontiguousarray(g.reshape(DEPTH * 3, 8, 128).transpose(2, 0, 1).reshape(128, DEPTH * 3 * 8))
    sh["final_norm"] = np.ascontiguousarray(inp["final_norm"], f).reshape(1, D)
    lam = np.stack([np.asarray(inp[k], f) for k in ("lambda_q1", "lambda_k1", "lambda_q2", "lambda_k2")], axis=1)
    sh["lam4"] = np.ascontiguousarray(lam.reshape(1, DEPTH * 4 * 64))
    sh["subln"] = np.ascontiguousarray(inp["subln_gain"], f).reshape(1, DEPTH * 128)
    sh["sink"] = np.ascontiguousarray(inp["sink_logits"], f).reshape(1, DEPTH * 8)
    t5 = np.asarray(inp["t5_table"], f)
    sh["t5flat"] = np.ascontiguousarray(t5.reshape(1, 384))
    sh["ident"] = np.eye(128, dtype=f)
    sel = np.zeros((128, 2, 128), f)
    for p in range(128):
        sel[p, p // 64, p] = 1.0
    sh["sel"] = sel.reshape(128, 256)
    sh["atoms"] = np.ascontiguousarray(build_atoms(np.asarray(inp["natten_rpb"], f)).reshape(DEPTH * 128, 122 * 64))
    bt = tab.fill(t5)
    full = np.zeros((128, 96, 128), f)
    full[:, :bt.shape[1], :] = bt
    sh["btab"] = full.reshape(128, 96 * 128)
    return sh


def kernel(**inputs):
    nseq = 2
    nc, tab = _get_program(nseq, DEPTH)
    sh = prep_shared(inputs, tab)
    x = np.ascontiguousarray(inputs["x"], np.float32)
    in_maps = []
    for c in range(8):
        m = dict(sh)
        m["x"] = x[c * nseq:(c + 1) * nseq].reshape(nseq * SEQ, D)
        in_maps.append(m)
    res = run_bass_kernel_spmd(nc, in_maps, core_ids=list(range(8)))
    out = np.concatenate([r["out"].reshape(nseq, SEQ, D) for r in res.results], axis=0)
    return out.astype(np.float32)
```

```python
import math
import numpy as np
import concourse.bass as bass
import concourse.mybir as mybir
from concourse.bass_utils import run_bass_kernel_spmd

F32 = mybir.dt.float32
BF16 = mybir.dt.bfloat16
AF = mybir.ActivationFunctionType
ALU = mybir.AluOpType

D = 1024; DFF = 2816; NM = 16; SEQ = 4096; L = SEQ + NM; DEPTH = 4
NVC = 4 * 129 + 8 * 65 + 2 * 65
VA0, VB0, VC0 = 0, 516, 1036
EPS = 1e-6
NEGM = -30000.0
NQK = 22


class Op:
    __slots__ = ("eng", "fn", "deps", "dma", "inc", "ms", "lane", "tgt", "lc")


class Sched:
    ENGS = ("pe", "act", "dve", "pool", "sp")

    def __init__(self):
        self.ops = {e: [] for e in self.ENGS}
        self.lastw = {}
        self.rd_eng = {}
        self.rd_dma = {}
        self.n = 0

    def add(self, eng, fn, reads=(), writes=(), dma=False):
        op = Op()
        op.lc = eng
        if eng == "cast":
            eng = "pool"
        op.eng = eng; op.fn = fn; op.dma = dma; op.inc = False; op.ms = 0; op.lane = 0; op.tgt = 0
        deps = []
        for r in reads:
            w = self.lastw.get(r)
            if w is not None:
                deps.append(w)
        for r in writes:
            w = self.lastw.get(r)
            if w is not None:
                deps.append(w)
            re = self.rd_eng.get(r)
            if re:
                deps.extend(re.values())
            rdm = self.rd_dma.get(r)
            if rdm:
                deps.extend(rdm)
        fd = []
        seen = set()
        for d in deps:
            if d is op or id(d) in seen:
                continue
            seen.add(id(d))
            if (not d.dma) and (not dma) and d.eng == "pe" and eng == "pe":
                continue
            if not d.dma:
                d.inc = True
            fd.append(d)
        op.deps = fd
        for r in reads:
            if dma:
                self.rd_dma.setdefault(r, []).append(op)
            else:
                self.rd_eng.setdefault(r, {})[eng] = op
        for r in writes:
            self.lastw[r] = op
            self.rd_eng[r] = {}
            self.rd_dma[r] = []
        self.ops[eng].append(op)
        self.n += 1
        return op

    def emit(self, nc, block, sems, lanes):
        for e in self.ENGS:
            c = 0
            for op in self.ops[e]:
                if op.inc and not op.dma:
                    c += 1
                    op.ms = c
        lane_cnt = {}
        li = {}
        for e in self.ENGS:
            for op in self.ops[e]:
                if op.dma:
                    ln = lanes[op.lc]
                    i = li.get(op.lc, 0)
                    li[op.lc] = i + 1
                    s = ln[i % len(ln)]
                    lane_cnt[s] = lane_cnt.get(s, 0) + 1
                    op.lane = s
                    op.tgt = 16 * lane_cnt[s]
        final = dict(lane_cnt)

        def run(e, eng):
            seen = {}

            def wait(sem, val):
                if seen.get(id(sem), 0) < val:
                    eng.wait_ge(sem, val)
                    seen[id(sem)] = val

            for op in self.ops[e]:
                for d in op.deps:
                    if d.dma:
                        wait(d.lane, d.tgt)
                    else:
                        wait(sems[d.eng], d.ms)
                if op.dma:
                    if op.tgt > 16:
                        wait(op.lane, op.tgt - 16)
                    op.fn(eng).then_inc(op.lane, 16)
                elif op.inc:
                    op.fn(eng).then_inc(sems[e], 1)
                else:
                    op.fn(eng)
            if e == "sp":
                for s, c in final.items():
                    wait(s, 16 * c)

        @block.tensor
        def _(eng):
            run("pe", eng)

        @block.scalar
        def _(eng):
            run("act", eng)

        @block.vector
        def _(eng):
            run("dve", eng)

        @block.gpsimd
        def _(eng):
            run("pool", eng)

        @block.sync
        def _(eng):
            run("sp", eng)


class Tile:
    pass


def make_tiles(nseq):
    tiles = []
    for s in range(nseq):
        for i in range(8):
            t = Tile()
            t.seq = s; t.idx = i
            if i == 0:
                t.pos0 = 0; t.n = 528
                t.blocks = [(0, 16)] + [(16 + 128 * j, 128) for j in range(4)]
                t.qb = [-1, 0, 1, 2, 3]
                t.chunks = [(0, 272), (272, 256)]
                t.groups = [[0, 1, 2], [3, 4]]
            else:
                t.pos0 = 16 + 512 * i; t.n = 512
                t.blocks = [(128 * j, 128) for j in range(4)]
                t.qb = [4 * i + j for j in range(4)]
                t.chunks = [(0, 512)]
                t.groups = [[0, 1, 2, 3]]
            t.row0 = s * L + t.pos0
            tiles.append(t)
    return tiles


def t5_bucket_np(rel):
    nb = 16; max_exact = 8
    ret = np.where(rel > 0, nb, 0)
    n = np.abs(rel)
    nf = np.maximum(n, 1).astype(np.float32)
    large = max_exact + (np.log(nf / max_exact) / np.float32(math.log(128 / max_exact))
                         * (nb - max_exact)).astype(np.int32)
    large = np.minimum(large, nb - 1)
    return ret + np.where(n < max_exact, n, large)


class BiasTab:
    def __init__(self):
        self.keys = {}
        self.specs = []

    def get(self, mixer, h, kpos0, nk, qpos0, nq):
        lo = kpos0 - (qpos0 + nq - 1); hi = (kpos0 + nk - 1) - qpos0
        kmeta = kpos0 < NM
        if lo >= 91:
            d = "P"
        elif hi <= -91:
            d = "N"
        else:
            d = kpos0 - qpos0
        if mixer == "C" and not kmeta:
            d = kpos0 - qpos0
        key = (mixer, h, nk, nq, d, kmeta if mixer == "C" else False)
        if key not in self.keys:
            self.keys[key] = len(self.specs)
            self.specs.append((mixer, h, kpos0, nk, qpos0, nq))
        return self.keys[key]

    def fill(self, t5):
        out = np.zeros((128, len(self.specs), 128), np.float32)
        for i, (mixer, h, kpos0, nk, qpos0, nq) in enumerate(self.specs):
            kp = kpos0 + np.arange(nk)[:, None]; qp = qpos0 + np.arange(nq)[None, :]
            rel = kp - qp
            b = t5_bucket_np(rel)
            if mixer == "A":
                v = t5[b, h]
            else:
                v = t5[b, 4 + h]
                vis = (kp < NM) | (np.abs(rel) <= 128)
                v = np.where(vis, v, np.float32(NEGM))
            out[:nk, i, :nq] = v
        return out


def build_atoms(rpb):
    cols = np.arange(64)
    cstart = np.clip(cols - 8, 0, 48)
    ok = (cols[None, :] >= cstart[:, None]) & (cols[None, :] < cstart[:, None] + 16)
    cidx = np.clip(cols[None, :] - cols[:, None] + 15, 0, 30)
    out = np.full((DEPTH, 128, 122, 64), np.float32(NEGM), np.float32)
    for l in range(DEPTH):
        for hd in range(8):
            for dr in range(15):
                g = rpb[l, hd, dr][cidx]
                a = np.where(ok, g, np.float32(NEGM)).T
                out[l, 0:64, hd * 15 + dr, :] = a
                out[l, 64:128, hd * 15 + dr, :] = a
        mq = np.full((64, 64), np.float32(NEGM), np.float32)
        mq[:16, :16] = 0.0
        out[l, 0:64, 121, :] = mq
        out[l, 64:128, 121, :] = mq
    return out


def build_program(nseq=2, depth=DEPTH, dbg=None):
    dbg = dbg or {}
    stages = dbg.get('stages', {'attnA', 'attnB', 'attnC', 'merge', 'ffn2', 'ffn1', 'proj'})
    nc = bass.Bass("TRN2", target_bir_lowering=False)
    S = Sched()
    tab = BiasTab()
    tiles = make_tiles(nseq)
    NTOK = nseq * L

    def din(name, shape, dt=F32):
        return nc.dram_tensor(name, list(shape), dt, kind="ExternalInput").ap()

    def dscr(name, shape, dt):
        return nc.dram_tensor(name, list(shape), dt, kind="Internal").ap()

    x_in = din("x", [nseq * SEQ, D])
    meta_in = din("meta", [NM, D])
    w_f1i = din("w_ffn1_in", [DEPTH * D, 2 * DFF]); w_f1o = din("w_ffn1_out", [DEPTH * DFF, D])
    w_f2i = din("w_ffn2_in", [DEPTH * D, 2 * DFF]); w_f2o = din("w_ffn2_out", [DEPTH * DFF, D])
    w_inp = din("w_in", [DEPTH * D, 3840]); w_gat = din("w_gate", [DEPTH * 3 * D, D])
    w_brn = din("w_branch", [DEPTH * 3 * 512, D]); w_out = din("w_out", [DEPTH * D, D])
    gcols_in = din("gcols", [128, 3 * DEPTH * 8])
    gfin_in = din("final_norm", [1, D])
    lam_in = din("lam4", [1, DEPTH * 4 * 64])
    subg_in = din("subln", [1, DEPTH * 128])
    sink_in = din("sink", [1, DEPTH * 8])
    t5_in = din("t5flat", [1, 384])
    ident_in = din("ident", [128, 128])
    sel_in = din("sel", [128, 256])
    atoms_in = din("atoms", [DEPTH * 128, 122 * 64])
    out_d = nc.dram_tensor("out", [nseq * SEQ, D], F32, kind="ExternalOutput").ap()

    b_f1i = dscr("b_f1i", [DEPTH * D, 2 * DFF], BF16); b_f1o = dscr("b_f1o", [DEPTH * DFF, D], BF16)
    b_f2i = dscr("b_f2i", [DEPTH * D, 2 * DFF], BF16); b_f2o = dscr("b_f2o", [DEPTH * DFF, D], BF16)
    b_inp = dscr("b_inp", [DEPTH * D, 3840], BF16); b_gat = dscr("b_gat", [DEPTH * 3 * D, D], BF16)
    b_brn = dscr("b_brn", [DEPTH * 3 * 512, D], BF16); b_out = dscr("b_out", [DEPTH * D, D], BF16)
    Hs = dscr("Hs", [NTOK, D], F32)
    QKs = dscr("QKs", [2 * nseq * NQK * 128, L], BF16)
    Vs = dscr("Vs", [2 * nseq * L, NVC], BF16)

    import contextlib
    es = contextlib.ExitStack()

    def sb(name, shape, dt):
        return es.enter_context(nc.sbuf_tensor("sb_" + name, list(shape), dt))

    NTAB = 96
    with es:
        h_tm = sb("h_tm", [128, 5, D], F32)
        junk = sb("junk", [128, 512], BF16)
        stat = sb("stat", [128, 64], F32)
        xhat = sb("xhat", [128, 2, D], BF16)
        xnT = sb("xnT", [128, 8, 528], BF16)
        big = sb("big", [128, 2 * 8448], BF16)
        wring = sb("wring", [128, 3, 4096], BF16)
        sg = sb("sg", [128, 2, 512], F32)
        qTa = sb("qTa", [128, 4, 528], BF16); qTb = sb("qTb", [128, 4, 528], BF16); qTc = sb("qTc", [128, 4, 528], BF16)
        kTb = sb("kTb", [128, 4, 16 + 1024], BF16); Vb = sb("Vb", [128, 9, 520], BF16)
        kTc = sb("kTc", [128, 2, 16 + 768], BF16); Vc = sb("Vc", [128, 7, 130], BF16)
        ptb = sb("ptb", [128, 3, 512], BF16)
        y_tm = sb("y_tm", [128, 5, 512], BF16)
        yT = sb("yT", [128, 12, 528], BF16)
        vst = sb("vst", [128, 2, NVC], BF16)
        o0n = sb("o0n", [128, 4, 128], F32)
        dA = sb("dA", [128, 4, 128], F32)
        ident = sb("ident", [128, 128], BF16)
        sel = sb("sel", [128, 2, 128], BF16)
        atoms = sb("atoms", [128, 122, 64], BF16)
        zt = sb("zt", [128, 128], BF16)
        gcols = sb("gcols", [128, 3 * DEPTH, 8], F32)
        lamb = sb("lamb", [128, 256], F32)
        btab = sb("btab", [128, NTAB, 128], BF16)
        lamt = sb("lamt", [128, 64], F32)
        lamc = sb("lamc", [128, 32], F32)
        subg = sb("subg", [128, DEPTH, 128], F32)
        sinkb = sb("sinkb", [128, DEPTH * 8], F32)
        t5b = sb("t5b", [128, 384], F32)

        ps = [es.enter_context(nc.psum_tensor(f"ps{i}", [128, 512], F32)) for i in range(6)]
        pst = [es.enter_context(nc.psum_tensor(f"pst{i}", [128, 1024], BF16)) for i in range(2)]
        sems = {e: es.enter_context(nc.semaphore(f"sem_{e}")) for e in Sched.ENGS}
        lanes = {"sp": [es.enter_context(nc.semaphore(f"lsp{i}")) for i in range(dbg.get("splanes", 8))],
                 "pool": [es.enter_context(nc.semaphore(f"lpl{i}")) for i in range(20)],
                 "cast": [es.enter_context(nc.semaphore(f"lca{i}")) for i in range(dbg.get("castlanes", 3))]}
        block = es.enter_context(nc.Block())

        hT = big[:, 0:22 * 528].rearrange("p (k n) -> p k n", k=22)
        mrg_acc = big[:, 0:4 * 528 * 2].bitcast(F32).rearrange("p (k n) -> p k n", k=4)
        mrgT = big[:, 4 * 528 * 2: 4 * 528 * 2 + 8 * 528].rearrange("p (k n) -> p k n", k=8)
        kva_k = [big[:, i * 8448: i * 8448 + L] for i in range(2)]
        kva_v = [big[:, i * 8448 + 4128: i * 8448 + 4128 + 33 * 129].rearrange("p (t c) -> p t c", c=129)
                 for i in range(2)]
        BIG = ["big", "kva0", "kva1"]
        qkst = yT[:, 0:8, :].rearrange("p (a c) n -> p a c n", a=2)
        YTA = ["yT0", "yT1", "yT2"]
        gfin = vst[:, :, :].rearrange("p a n -> p (a n)")[:, 0:2 * D].bitcast(F32)

        def mm(out, lhsT, rhs, start, stop, R, W):
            S.add("pe", lambda e: e.matmul(out, lhsT, rhs, start=start, stop=stop, skip_group_check=True), R, W)

        def tr(out, in_, idn, R, W):
            S.add("pe", lambda e: e.transpose(out, in_, idn), R, W)

        def act(out, in_, func, R, W, bias=None, scale=None, accum=None):
            kw = {}
            if bias is not None: kw["bias"] = bias
            if scale is not None: kw["scale"] = scale
            if accum is not None: kw["accum_out"] = accum
            S.add("act", lambda e: e.activation(out=out, in_=in_, func=func, **kw), R, W)

        def dma(out, in_, R, W, q="sp"):
            S.add(q, lambda e: e.dma_start(out=out, in_=in_), R, W, dma=True)

        def dma_nc(out, in_, R, W, q="sp"):
            S.add(q, lambda e: e.dma_start(out=out, in_=in_, allow_slow_non_contiguous=True), R, W, dma=True)

        def vts(out, in0, s1, s2, op0, op1, R, W):
            if s2 is None:
                S.add("dve", lambda e: e.tensor_scalar(out=out, in0=in0, scalar1=s1, scalar2=None, op0=op0), R, W)
            else:
                S.add("dve", lambda e: e.tensor_scalar(out=out, in0=in0, scalar1=s1, scalar2=s2, op0=op0, op1=op1), R, W)

        def vtt(out, in0, in1, op, R, W):
            S.add("dve", lambda e: e.tensor_tensor(out=out, in0=in0, in1=in1, op=op), R, W)

        def vstt(out, in0, sc, in1, op0, op1, R, W):
            S.add("dve", lambda e: e.scalar_tensor_tensor(out=out, in0=in0, scalar=sc, in1=in1, op0=op0, op1=op1), R, W)

        def vrec(out, in_, R, W):
            S.add("dve", lambda e: e.reciprocal(out, in_), R, W)

        def vcopy(out, in_, R, W):
            S.add("dve", lambda e: e.tensor_copy(out, in_), R, W)

        def vmemset(ap, val, R, W):
            S.add("dve", lambda e: e.memset(ap, val), R, W)

        dma(ident[:], ident_in[:, :], [], ["ident"], q="cast")
        dma(sel[:].rearrange("p a m -> p (a m)"), sel_in[:, :], [], ["sel"], q="cast")
        dma(gcols[:].rearrange("p a k -> p (a k)"), gcols_in[:, :], [], ["gcols"])
        if "bcast" not in dbg.get("skip", ()):
            dma(subg[:].rearrange("p a k -> p (a k)"), subg_in.partition_broadcast(128), [], ["subg"])
            dma(sinkb[:], sink_in.partition_broadcast(128), [], ["sinkb"])
            dma(t5b[:], t5_in.partition_broadcast(128), [], ["t5b"])
        vmemset(vst[:], 1.0, [], ["vst0", "vst1"])
        vmemset(zt[:], 0.0, [], ["zt"])
        act(sinkb[:], sinkb[:], AF.Exp, ["sinkb"], ["sinkb"])

        WT = {}

        def cast_jobs(src, dst, rows, row0, name):
            jobs = []
            for r in range(0, rows, 128):
                rr = min(128, rows - r)
                jobs.append((dst[row0 + r: row0 + r + rr, :], src[row0 + r: row0 + r + rr, :], f"{name}_{r}"))
                WT.setdefault(name, []).append(f"{name}_{r}")
            return jobs

        def jobs_a(l):
            return (cast_jobs(w_f1i, b_f1i, D, l * D, f"Wf1i{l}") + cast_jobs(w_f1o, b_f1o, DFF, l * DFF, f"Wf1o{l}")
                    + cast_jobs(w_inp, b_inp, D, l * D, f"Winp{l}"))

        def jobs_b(l):
            return (cast_jobs(w_gat, b_gat, 3 * D, l * 3 * D, f"Wgat{l}") + cast_jobs(w_brn, b_brn, 3 * 512, l * 3 * 512, f"Wbrn{l}")
                    + cast_jobs(w_out, b_out, D, l * D, f"Wout{l}")
                    + cast_jobs(w_f2i, b_f2i, D, l * D, f"Wf2i{l}") + cast_jobs(w_f2o, b_f2o, DFF, l * DFF, f"Wf2o{l}"))

        def issue_casts(jobs):
            if 'casts' in dbg.get('skip', ()):
                return
            for (o, i, name) in jobs:
                dma(o, i, [], [name], q="cast")

        wr_state = {"i": 0}

        def wslot():
            i = wr_state["i"] % 3
            wr_state["i"] += 1
            return i

        def wload_cols(bw, row0, nk, c0, ncols, name):
            s = wslot()
            v = wring[:, s, 0:nk * ncols].rearrange("p (k c) -> p k c", k=nk)
            dma(v, bw[row0: row0 + nk * 128, c0: c0 + ncols].rearrange("(k p) c -> p k c", p=128), WT[name], [f"w{s}"])
            return s, v

        def hb(bi):
            return f"h{bi}"

        def norm_stage(t, gidx):
            nb = len(t.blocks)
            for bi, (off, bs) in enumerate(t.blocks):
                act(xhat[:bs, bi % 2, :], h_tm[:bs, bi, :], AF.Square, [hb(bi)], [f"ss{bi}", f"xhat{bi % 2}"], accum=stat[:bs, bi:bi + 1])
            for bi, (off, bs) in enumerate(t.blocks):
                act(stat[:bs, 8 + bi:9 + bi], stat[:bs, bi:bi + 1], AF.Sqrt, [f"ss{bi}"], [f"sd{bi}"], scale=1.0 / D, bias=EPS)
            for bi, (off, bs) in enumerate(t.blocks):
                vrec(stat[:bs, 16 + bi:17 + bi], stat[:bs, 8 + bi:9 + bi], [f"sd{bi}"], [f"rs{bi}"])
            for bi, (off, bs) in enumerate(t.blocks):
                xi = bi % 2
                if xi == 1:
                    act(xhat[:bs, xi, :], h_tm[:bs, bi, :], AF.Copy, [hb(bi), f"rs{bi}"], [f"xhat{xi}"], scale=stat[:bs, 16 + bi:17 + bi])
                else:
                    vts(xhat[:bs, xi, :], h_tm[:bs, bi, :], stat[:bs, 16 + bi:17 + bi], None, ALU.mult, None,
                        [hb(bi), f"rs{bi}"], [f"xhat{xi}"])
                pv = pst[xi][:, :].rearrange("p (k n) -> p k n", k=8)
                for k in range(8):
                    tr(pv[:, k, :bs], xhat[:bs, xi, k * 128:(k + 1) * 128], ident[:bs, :bs],
                       [f"xhat{xi}", "ident"], [f"pst{xi}"])
                vtt(xnT[:, :, off:off + bs], pv[:, :, :bs], gcols[:, gidx, :].unsqueeze(2).to_broadcast([128, 8, bs]),
                    ALU.mult, [f"pst{xi}", "gcols"], [f"xn{bi}"])

        def xn_tokens(t):
            return [f"xn{bi}" for bi in range(len(t.blocks))]

        def ffn_stage(t, l, which):
            bi_w, bo_w = (b_f1i, b_f1o) if which == 1 else (b_f2i, b_f2o)
            ni, no = (f"Wf1i{l}", f"Wf1o{l}") if which == 1 else (f"Wf2i{l}", f"Wf2o{l}")
            norm_stage(t, l * 3 + (0 if which == 1 else 2))
            XN = xn_tokens(t)
            cnt = 0
            for j in range(11):
                s = wslot()
                v = wring[:, s, :].rearrange("p (k c) -> p k c", k=8)
                dma(v[:, :, 0:256], bi_w[l * D:(l + 1) * D, j * 256:(j + 1) * 256].rearrange("(k p) c -> p k c", p=128),
                    WT[ni], [f"w{s}"])
                dma(v[:, :, 256:512], bi_w[l * D:(l + 1) * D, DFF + j * 256: DFF + (j + 1) * 256].rearrange("(k p) c -> p k c", p=128),
                    WT[ni], [f"w{s}"])
                for c in range(2):
                    fc = j * 2 + c
                    for (off, cs) in t.chunks:
                        pg = ps[(cnt % 3) * 2]; pu = ps[(cnt % 3) * 2 + 1]
                        tg = f"ps{(cnt % 3) * 2}"; tu = f"ps{(cnt % 3) * 2 + 1}"
                        si = cnt % 2
                        hw = (BIG if cnt == 0 else []) + [f"hT{fc}"]
                        cnt += 1
                        for k in range(8):
                            mm(pg[:, :cs], v[:, k, c * 128:(c + 1) * 128], xnT[:, k, off:off + cs], k == 0, k == 7,
                               [f"w{s}"] + XN, [tg])
                        for k in range(8):
                            mm(pu[:, :cs], v[:, k, 256 + c * 128:256 + (c + 1) * 128], xnT[:, k, off:off + cs], k == 0, k == 7,
                               [f"w{s}"] + XN, [tu])
                        act(sg[:, si, :cs], pg[:, :cs], AF.Silu, [tg], [f"sg{si}"])
                        vtt(hT[:, fc, off:off + cs], sg[:, si, :cs], pu[:, :cs], ALU.mult, [f"sg{si}", tu], hw)
            nb = len(t.blocks)
            for half in range(2):
                banks = [1 + bi for bi in range(nb)]
                for (k0, nk) in ((0, 8), (8, 8), (16, 6)):
                    s, v = wload_cols(bo_w, l * DFF + k0 * 128, nk, half * 512, 512, no)
                    for bi, (off, bs) in enumerate(t.blocks):
                        for kk in range(nk):
                            k = k0 + kk
                            mm(ps[banks[bi]][:bs, :], hT[:, k, off:off + bs], v[:, kk, :], k == 0, k == 21,
                               [f"w{s}", f"hT{k}"] + BIG, [f"ps{banks[bi]}"])
                for bi, (off, bs) in enumerate(t.blocks):
                    vstt(h_tm[:bs, bi, half * 512:(half + 1) * 512], ps[banks[bi]][:bs, :], 0.5,
                         h_tm[:bs, bi, half * 512:(half + 1) * 512], ALU.mult, ALU.add, [f"ps{banks[bi]}", hb(bi)], [hb(bi)])

        def proj_stage(t, l, par):
            norm_stage(t, l * 3 + 1)
            XN = xn_tokens(t)
            nm = f"Winp{l}"
            qkbase = (par * nseq + t.seq) * NQK * 128
            fm = [(0, 4, 0.125, 0), (512, 4, None, 4), (1536, 4, 0.125, 8), (2048, 4, None, 12), (3072, 4, 0.125, 16)]
            cnt = 0
            for gi, (c0, nch, scl, ch0) in enumerate(fm + [(None, 2, None, 20)]):
                si = gi % 2
                if c0 is not None:
                    s, v = wload_cols(b_inp, l * D, 8, c0, 512, nm)
                else:
                    s = wslot()
                    v = wring[:, s, :].rearrange("p (k c) -> p k c", k=8)
                    src = b_inp[l * D:(l + 1) * D, :].rearrange("(k p) c -> p k c", p=128)
                    dma_nc(v[:, :, 0:128], src[:, :, 3584:3712], WT[nm], [f"w{s}"])
                    dma_nc(v[:, :, 128:192], src[:, :, 3648:3712], WT[nm], [f"w{s}"])
                    dma_nc(v[:, :, 192:256], src[:, :, 3584:3648], WT[nm], [f"w{s}"])
                    nch, scl, ch0 = 2, None, 20
                for c in range(nch):
                    for (off, cs) in t.chunks:
                        b = cnt % 4; cnt += 1
                        for k in range(8):
                            mm(ps[b][:, :cs], v[:, k, c * 128:(c + 1) * 128], xnT[:, k, off:off + cs], k == 0, k == 7,
                               [f"w{s}"] + XN, [f"ps{b}"])
                        if scl is not None:
                            act(qkst[:, si, c, off:off + cs], ps[b][:, :cs], AF.Copy, [f"ps{b}"], [f"qkst{si}"] + YTA, scale=scl)
                        else:
                            vcopy(qkst[:, si, c, off:off + cs], ps[b][:, :cs], [f"ps{b}"], [f"qkst{si}"] + YTA)
                dst = QKs[qkbase + ch0 * 128: qkbase + (ch0 + nch) * 128, t.pos0:t.pos0 + t.n].rearrange("(c p) n -> p c n", p=128)
                dma(dst, qkst[:, si, 0:nch, 0:t.n], [f"qkst{si}"], [f"QK{par}_{t.seq}_{t.idx}"], q="sp")
            vsl = []
            for (c0, ncols) in ((1024, 512), (2560, 512)):
                vsl.append(wload_cols(b_inp, l * D, 8, c0, ncols, nm))
            s3, v3 = wload_cols(b_inp, l * D, 8, 3712, 128, nm)
            vbase = (par * nseq + t.seq) * L
            for bi, (off, bs) in enumerate(t.blocks):
                vi = bi % 2
                for gi2 in range(3):
                    b = 2 + (bi * 3 + gi2) % 4
                    if gi2 < 2:
                        s, v = vsl[gi2]; ncols = 512
                    else:
                        s, v = s3, v3; ncols = 128
                    for k in range(8):
                        mm(ps[b][:bs, :ncols], xnT[:, k, off:off + bs], v[:, k, :ncols], k == 0, k == 7,
                           [f"w{s}"] + XN, [f"ps{b}"])
                    if gi2 == 0:
                        dv = vst[:bs, vi, VA0:VA0 + 516].rearrange("p (h c) -> p h c", c=129)[:, :, 0:128]
                        sv = ps[b][:bs, :512].rearrange("p (h c) -> p h c", c=128)
                    elif gi2 == 1:
                        dv = vst[:bs, vi, VB0:VB0 + 520].rearrange("p (h c) -> p h c", c=65)[:, :, 0:64]
                        sv = ps[b][:bs, :512].rearrange("p (h c) -> p h c", c=64)
                    else:
                        dv = vst[:bs, vi, VC0:VC0 + 130].rearrange("p (h c) -> p h c", c=65)[:, :, 0:64]
                        sv = ps[b][:bs, :128].rearrange("p (h c) -> p h c", c=64)
                    if gi2 == 1:
                        S.add("act", (lambda dv=dv, sv=sv: (lambda e: e.copy(dv, sv)))(), [f"ps{b}"], [f"vst{vi}"])
                    else:
                        vcopy(dv, sv, [f"ps{b}"], [f"vst{vi}"])
                dma(Vs[vbase + t.pos0 + off: vbase + t.pos0 + off + bs, :], vst[:bs, vi, :], [f"vst{vi}"],
                    [f"V{par}_{t.seq}_{t.idx}"], q="sp")

        def seq_tokens(kind, par, seq):
            return [f"{kind}{par}_{seq}_{i}" for i in range(8)]

        def qpos_of(t, bi):
            return t.pos0 + t.blocks[bi][0]

        st_state = {"i": 0}

        def next_st():
            i = st_state["i"] % 2
            st_state["i"] += 1
            p = st_state["i"] % 3
            return i, p

        def transpose_y(t, branch):
            if dbg.get("dump_y") == branch:
                for bi, (off, bs) in enumerate(t.blocks):
                    vts(h_tm[:bs, bi, 0:512], y_tm[:bs, bi, :], 1.0, None, ALU.mult, None, [f"ytm{bi}", hb(bi)], [hb(bi)])
            for bi, (off, bs) in enumerate(t.blocks):
                xi = bi % 2
                pv = pst[xi][:, 0:512].rearrange("p (k n) -> p k n", k=4)
                for c in range(4):
                    tr(pv[:, c, :bs], y_tm[:bs, bi, c * 128:(c + 1) * 128], ident[:bs, :bs], [f"ytm{bi}", "ident"], [f"pst{xi}"])
                vts(yT[:, branch * 4:(branch + 1) * 4, off:off + bs], pv[:, :, :bs], 1.0, None, ALU.mult, None, [f"pst{xi}"], [f"yT{branch}", "qkst0", "qkst1"])

        def attn_A(t, l, par):
            qkbase = (par * nseq + t.seq) * NQK * 128
            vbase = (par * nseq + t.seq) * L
            QKT = seq_tokens("QK", par, t.seq); VT = seq_tokens("V", par, t.seq)
            lcol = lamc[:, l: l + 1]
            for h in range(4):
                ri = h % 2
                kk = kva_k[ri]; vv = kva_v[ri]
                r0 = qkbase + (4 + h) * 128
                KV = [f"kva{ri}"]
                dma(kk, QKs[r0:r0 + 128, :], QKT, KV, q="sp")
                dma_nc(vv[:16, 0, :], Vs[vbase: vbase + 16, VA0 + h * 129: VA0 + (h + 1) * 129], VT, KV, q="sp")
                for t4 in range(8):
                    dma_nc(vv[:, 1 + 4 * t4: 5 + 4 * t4, :],
                           Vs[vbase + 16 + 512 * t4: vbase + 16 + 512 * (t4 + 1), VA0 + h * 129: VA0 + (h + 1) * 129].rearrange("(t p) c -> p t c", p=128),
                           VT, KV, q="sp")
                for grp in (t.groups if dbg.get('a_mode', 2) >= 1 else []):
                    segs = [(bi, t.blocks[bi][0], t.blocks[bi][1]) for bi in grp]
                    g0 = segs[0][1]; gn = sum(sg_[2] for sg_ in segs)
                    order = sorted(range(len(segs)), key=lambda i: -segs[i][2])
                    for m in range(2):
                        ob = [2 + 2 * m, 3 + 2 * m]
                        first = {ob[0]: True, ob[1]: True}
                        pend = None
                        for kt in range(34):
                            cur = None
                            if kt < 33:
                                nk = 16 if kt == 0 else 128
                                kp0 = 0 if kt == 0 else 16 + 128 * (kt - 1)
                                sti, pti = next_st()
                                stb = ps[sti]
                                lo = kp0 - (t.pos0 + g0 + gn - 1); hi = (kp0 + nk - 1) - (t.pos0 + g0)
                                far = "P" if lo >= 91 else ("N" if hi <= -91 else None)
                                mm(stb[:nk, :gn], kk[m * 64:(m + 1) * 64, kp0:kp0 + nk], qTa[m * 64:(m + 1) * 64, h, g0:g0 + gn],
                                   True, far is not None, KV + ["qTa"], [f"ps{sti}"])
                                if far is None:
                                    for si_, (bi, off, bs) in enumerate(segs):
                                        ti = tab.get("A", h, kp0, nk, t.pos0 + off, bs)
                                        mm(stb[:nk, off - g0: off - g0 + bs], ident[:, :nk], btab[:, ti, :bs], False,
                                           si_ == len(segs) - 1, ["ident", "btab"], [f"ps{sti}"])
                                    act(ptb[:nk, pti, :gn], stb[:nk, :gn], AF.Exp, [f"ps{sti}"], [f"pt{pti}"])
                                else:
                                    col = (31 if far == "P" else 15) * 12 + h
                                    act(ptb[:nk, pti, :gn], stb[:nk, :gn], AF.Exp, [f"ps{sti}", "t5b"], [f"pt{pti}"],
                                        bias=t5b[:nk, col:col + 1])
                                cur = (kt, nk, pti)
                            if pend is not None and dbg.get('a_mode', 2) >= 1.5:
                                pkt, pnk, ppt = pend
                                for oi in order:
                                    bi, off, bs = segs[oi]
                                    bank = ob[oi // 2]; c0 = (oi % 2) * 129
                                    mm(ps[bank][:bs, c0:c0 + 129], ptb[:pnk, ppt, off - g0: off - g0 + bs], vv[:pnk, pkt, :],
                                       first[bank], pkt == 32, [f"pt{ppt}"] + KV, [f"ps{bank}"])
                                    first[bank] = False
                            pend = cur
                        if dbg.get('a_mode', 2) < 1.6:
                            continue
                        for oi, (bi, off, bs) in enumerate(segs):
                            bank = ob[oi // 2]; c0 = (oi % 2) * 129
                            vrec(stat[:bs, 24 + m * 4 + oi: 25 + m * 4 + oi], ps[bank][:bs, c0 + 128:c0 + 129], [f"ps{bank}"], [f"rz{m}_{oi}"])
                        if m == 0:
                            for oi, (bi, off, bs) in enumerate(segs):
                                bank = ob[oi // 2]; c0 = (oi % 2) * 129
                                vts(o0n[:bs, oi, :], ps[bank][:bs, c0:c0 + 128], stat[:bs, 24 + oi:25 + oi], None, ALU.mult, None,
                                    [f"ps{bank}", f"rz0_{oi}"], [f"o0n{oi}"])
                        else:
                            for oi, (bi, off, bs) in enumerate(segs):
                                vts(stat[:bs, 32 + oi:33 + oi], stat[:bs, 28 + oi:29 + oi], lcol[:bs, :], None, ALU.mult, None,
                                    [f"rz1_{oi}", "lamc"], [f"c1_{oi}"])
                            for oi, (bi, off, bs) in enumerate(segs):
                                bank = ob[oi // 2]; c0 = (oi % 2) * 129
                                vstt(dA[:bs, oi, :], ps[bank][:bs, c0:c0 + 128], stat[:bs, 32 + oi:33 + oi], o0n[:bs, oi, :],
                                     ALU.mult, ALU.add, [f"ps{bank}", f"c1_{oi}", f"o0n{oi}"], [f"dA{oi}"])
                    if dbg.get('a_mode', 2) < 1.8:
                        continue
                    for oi, (bi, off, bs) in enumerate(segs):
                        act(junk[:bs, oi * 128:(oi + 1) * 128], dA[:bs, oi, :], AF.Square, [f"dA{oi}"], [f"ssA{oi}", f"junk{oi}"], accum=stat[:bs, 36 + oi:37 + oi])
                    for oi, (bi, off, bs) in enumerate(segs):
                        act(stat[:bs, 40 + oi:41 + oi], stat[:bs, 36 + oi:37 + oi], AF.Sqrt, [f"ssA{oi}"], [f"sdA{oi}"], scale=1.0 / 128, bias=EPS)
                    for oi, (bi, off, bs) in enumerate(segs):
                        vrec(stat[:bs, 44 + oi:45 + oi], stat[:bs, 40 + oi:41 + oi], [f"sdA{oi}"], [f"rsA{oi}"])
                    for oi, (bi, off, bs) in enumerate(segs):
                        vstt(y_tm[:bs, bi, h * 128:(h + 1) * 128], dA[:bs, oi, :], stat[:bs, 44 + oi:45 + oi], subg[:bs, l, :],
                             ALU.mult, ALU.mult, [f"dA{oi}", f"rsA{oi}", "subg"], [f"ytm{bi}"])
            if dbg.get('a_mode', 2) >= 2:
                transpose_y(t, 0)

        def attn_BC(t, l, par, mixer):
            qkbase = (par * nseq + t.seq) * NQK * 128
            vbase = (par * nseq + t.seq) * L
            QKT = seq_tokens("QK", par, t.seq); VT = seq_tokens("V", par, t.seq)
            if mixer == "B":
                r0 = 8 * t.idx
                rlo = max(0, r0 - 4); rhi = min(64, r0 + 12)
                nrow = rhi - rlo
                src = QKs[qkbase + 12 * 128: qkbase + 16 * 128, :].rearrange("(c p) n -> p c n", p=128)
                dma(kTb[:, :, 0:16], src[:, :, 0:16], QKT, ["kTb"], q="sp")
                for c4 in range(4):
                    dma(kTb[:, c4, 16:16 + nrow * 64], src[:, c4, 16 + rlo * 64: 16 + rhi * 64], QKT, ["kTb"], q="sp")
                dma(Vb[:16, 0, :], Vs[vbase: vbase + 16, VB0:VB0 + 520], VT, ["Vb"], q="sp")
                for t2 in range(0, nrow // 2, 2):
                    dma(Vb[:, 1 + t2: 3 + t2, :], Vs[vbase + 16 + rlo * 64 + t2 * 128: vbase + 16 + rlo * 64 + (t2 + 2) * 128, VB0:VB0 + 520].rearrange("(t p) c -> p t c", p=128),
                        VT, ["Vb"], q="sp")
                qT = qTb; kT = kTb; Vt = Vb; KR = ["kTb", "qTb"]; VR = ["Vb"]
            else:
                j0 = t.qb[1] if t.idx == 0 else t.qb[0]
                jlo = max(0, j0 - 1); jhi = min(32, j0 + 5)
                nblk = jhi - jlo
                src = QKs[qkbase + 20 * 128: qkbase + 22 * 128, :].rearrange("(c p) n -> p c n", p=128)
                dma(kTc[:, :, 0:16], src[:, :, 0:16], QKT, ["kTc"], q="sp")
                dma(kTc[:, :, 16:16 + nblk * 128], src[:, :, 16 + jlo * 128: 16 + jhi * 128], QKT, ["kTc"], q="sp")
                dma(Vc[:16, 0, :], Vs[vbase: vbase + 16, VC0:VC0 + 130], VT, ["Vc"], q="sp")
                for t2 in range(0, nblk, 2):
                    n2 = min(2, nblk - t2)
                    dma_nc(Vc[:, 1 + t2: 1 + t2 + n2, :], Vs[vbase + 16 + (jlo + t2) * 128: vbase + 16 + (jlo + t2 + n2) * 128, VC0:VC0 + 130].rearrange("(t p) c -> p t c", p=128),
                           VT, ["Vc"], q="sp")
                qT = qTc; kT = kTc; Vt = Vc; KR = ["kTc", "qTc"]; VR = ["Vc"]
            job = 0
            for bi, (off, bs) in enumerate(t.blocks if dbg.get("b_mode", 2) >= 1 else []):
                jb = t.qb[bi]
                if "b_blocks" in dbg and bi not in dbg["b_blocks"]:
                    continue
                kts = [(16, 0, 0, ("meta",))]
                if mixer == "B":
                    if jb < 0:
                        rows = [(0, "mq"), (0, "mq")]
                        tl = range(0, 4)
                    else:
                        r = 2 * jb
                        rs0 = min(max(r - 4, 0), 56); rs1 = min(max(r + 1 - 4, 0), 56)
                        tl = range(rs0 // 2, (rs1 + 7) // 2 + 1)
                    for tt in tl:
                        kts.append((128, 16 + (2 * tt - rlo) * 64, 1 + (2 * tt - rlo) // 2, ("rows", tt)))
                else:
                    if jb < 0:
                        bl = [0]
                    else:
                        bl = [j for j in (jb - 1, jb, jb + 1) if 0 <= j < 32]
                    for j in bl:
                        kts.append((128, 16 + (j - jlo) * 128, 1 + (j - jlo), ("blk", j)))
                if "b_maxkt" in dbg:
                    kts = kts[:dbg["b_maxkt"]]
                if "b_kts" in dbg:
                    kts = [kts[i] for i in dbg["b_kts"] if i < len(kts)]
                for hg in range(dbg.get("b_nhg", 2)):
                    ob = 2 + (job % 4); job += 1
                    first = True
                    pend = None
                    for ki in range(len(kts) + 1):
                        cur = None
                        if ki < len(kts):
                            nk, kc0, vti, kinfo = kts[ki]
                            sti, pti = next_st()
                            stb = ps[sti]
                            firstst = True
                            for s_ in range(4):
                                hd = hg * 4 + s_
                                ch = hd // 2; half = hd % 2
                                if mixer == "B":
                                    ksel = kT[half * 64:(half + 1) * 64, ch, kc0:kc0 + nk]
                                else:
                                    kv = hg
                                    ksel = kT[half * 64:(half + 1) * 64, 0 if half == kv else 1, kc0:kc0 + nk]
                                nbias = 0
                                blist = []
                                if mixer == "B" and kinfo[0] == "meta":
                                    blist.append((ident[:, :nk], zt[:, :bs], s_ * 128, bs))
                                if mixer == "B" and kinfo[0] == "rows":
                                    tt = kinfo[1]
                                    if jb < 0:
                                        for a in range(2):
                                            blist.append((sel[:, a, :], atoms[:, 121, 0:16], s_ * 128, 16))
                                    else:
                                        r = 2 * jb
                                        for a in range(2):
                                            for b_ in range(2):
                                                rq = r + b_; kr = 2 * tt + a
                                                rs = min(max(rq - 4, 0), 56)
                                                vis = rs <= kr < rs + 8
                                                ai = hd * 15 + (kr - rq + 7) if vis else 120
                                                blist.append((sel[:, a, :], atoms[:, ai, :], s_ * 128 + b_ * 64, 64))
                                if mixer == "C" and not dbg.get("c_nobias"):
                                    kp0 = 0 if kinfo[0] == "meta" else 16 + 128 * kinfo[1]
                                    ti = tab.get("C", hd, kp0, nk, t.pos0 + off, bs)
                                    blist.append((ident[:, :nk], btab[:, ti, :bs], s_ * 128, bs))
                                mm(stb[:nk, s_ * 128: s_ * 128 + bs], ksel, qT[half * 64:(half + 1) * 64, ch, off:off + bs],
                                   firstst, len(blist) == 0 and s_ == 3, KR, [f"ps{sti}"])
                                firstst = False
                                for bj, (lh, rh, cc0, cn) in enumerate(blist):
                                    mm(stb[:nk, cc0:cc0 + cn], lh, rh, False, (s_ == 3 and bj == len(blist) - 1),
                                       ["sel", "atoms", "ident", "btab", "zt"], [f"ps{sti}"])
                            if bs == 128:
                                act(ptb[:nk, pti, :], stb[:nk, :], AF.Exp, [f"ps{sti}"], [f"pt{pti}"])
                            else:
                                act(ptb[:nk, pti, :].rearrange("p (s c) -> p s c", c=128)[:, :, :bs],
                                    stb[:nk, :].rearrange("p (s c) -> p s c", c=128)[:, :, :bs], AF.Exp, [f"ps{sti}"], [f"pt{pti}"])
                            cur = (nk, vti, pti)
                        if pend is not None and dbg.get("b_mode", 2) >= 1.5:
                            pnk, pvti, ppt = pend
                            for s_ in range(4):
                                hd = hg * 4 + s_
                                vcol = hd * 65 if mixer == "B" else hg * 65
                                mm(ps[ob][:bs, s_ * 65:(s_ + 1) * 65], ptb[:pnk, ppt, s_ * 128: s_ * 128 + bs], Vt[:pnk, pvti, vcol:vcol + 65],
                                   first, ki == len(kts) and s_ == 3, [f"pt{ppt}"] + VR, [f"ps{ob}"])
                                first = False
                        pend = cur
                    if dbg.get("b_mode", 2) < 2:
                        continue
                    ov = ps[ob][:bs, 0:260].rearrange("p (s c) -> p s c", c=65)
                    zc = 48 + hg * 4
                    if mixer == "C":
                        vtt(stat[:bs, zc:zc + 4], ov[:, :, 64], sinkb[:bs, l * 8 + hg * 4: l * 8 + hg * 4 + 4], ALU.add,
                            [f"ps{ob}", "sinkb"], [f"zc{hg}"])
                        vrec(stat[:bs, zc + 8:zc + 12], stat[:bs, zc:zc + 4], [f"zc{hg}"], [f"rzb{hg}"])
                    else:
                        vts(stat[:bs, zc:zc + 4], ov[:, :, 64], 1.0, None, ALU.mult, None, [f"ps{ob}"], [f"zc{hg}"])
                        vrec(stat[:bs, zc + 8:zc + 12], stat[:bs, zc:zc + 4], [f"zc{hg}"], [f"rzb{hg}"])
                    vtt(y_tm[:bs, bi, hg * 256:(hg + 1) * 256].rearrange("p (s c) -> p s c", c=64), ov[:, :, 0:64],
                        stat[:bs, zc + 8:zc + 12].unsqueeze(2).to_broadcast([bs, 4, 64]), ALU.mult, [f"ps{ob}", f"rzb{hg}"], [f"ytm{bi}"])
            if dbg.get("b_mode", 2) >= 2:
                transpose_y(t, 1 if mixer == "B" else 2)

        def load_q(t, par):
            qkbase = (par * nseq + t.seq) * NQK * 128
            for (buf, ch0, nm) in ((qTa, 0, "qTa"), (qTb, 8, "qTb"), (qTc, 16, "qTc")):
                src = QKs[qkbase + ch0 * 128: qkbase + (ch0 + 4) * 128, t.pos0:t.pos0 + t.n].rearrange("(c p) n -> p c n", p=128)
                dma(buf[:, :, 0:t.n], src, [f"QK{par}_{t.seq}_{t.idx}"], [nm], q="sp")

        def merge_stage(t, l):
            norm_stage(t, l * 3 + 1)
            XN = xn_tokens(t)
            cnt = 0
            for ch in range(2):
                for i in range(3):
                    sgt, vg = wload_cols(b_gat, (l * 3 + i) * D, 8, ch * 512, 512, f"Wgat{l}")
                    sbt, vb = wload_cols(b_brn, (l * 3 + i) * 512, 4, ch * 512, 512, f"Wbrn{l}")
                    for c in range(4):
                        for (off, cs) in t.chunks:
                            b = (cnt % 3) * 2; cnt += 1
                            si = (cnt % 2)
                            for k in range(8):
                                mm(ps[b][:, :cs], vg[:, k, c * 128:(c + 1) * 128], xnT[:, k, off:off + cs], k == 0, k == 7,
                                   [f"w{sgt}"] + XN, [f"ps{b}"])
                            for k in range(4):
                                mm(ps[b + 1][:, :cs], vb[:, k, c * 128:(c + 1) * 128], yT[:, i * 4 + k, off:off + cs], k == 0, k == 3,
                                   [f"w{sbt}", f"yT{i}"], [f"ps{b + 1}"])
                            act(sg[:, si, :cs], ps[b][:, :cs], AF.Sigmoid, [f"ps{b}"], [f"sg{si}"])
                            if i == 0:
                                vtt(mrg_acc[:, c, off:off + cs], sg[:, si, :cs], ps[b + 1][:, :cs], ALU.mult, [f"sg{si}", f"ps{b + 1}"], BIG)
                            else:
                                vtt(sg[:, si, :cs], sg[:, si, :cs], ps[b + 1][:, :cs], ALU.mult, [f"sg{si}", f"ps{b + 1}"], [f"sg{si}"])
                                if i == 1:
                                    vtt(mrg_acc[:, c, off:off + cs], mrg_acc[:, c, off:off + cs], sg[:, si, :cs], ALU.add, [f"sg{si}"] + BIG, BIG)
                                else:
                                    vtt(mrgT[:, ch * 4 + c, off:off + cs], mrg_acc[:, c, off:off + cs], sg[:, si, :cs], ALU.add, [f"sg{si}"] + BIG, BIG)
            for half in range(2):
                s, v = wload_cols(b_out, l * D, 8, half * 512, 512, f"Wout{l}")
                for bi, (off, bs) in enumerate(t.blocks):
                    b = 1 + bi
                    for k in range(8):
                        mm(ps[b][:bs, :], mrgT[:, k, off:off + bs], v[:, k, :], k == 0, k == 7, [f"w{s}"] + BIG, [f"ps{b}"])
                    vtt(h_tm[:bs, bi, half * 512:(half + 1) * 512], ps[b][:bs, :], h_tm[:bs, bi, half * 512:(half + 1) * 512], ALU.add,
                        [f"ps{b}", hb(bi)], [hb(bi)])

        def layer_setup(l):
            lam_init = 0.8 - 0.6 * math.exp(-0.3 * l)
            b0 = 0
            dma(lamb[:], lam_in[:, l * 256:(l + 1) * 256].partition_broadcast(128), [], ["lamb"])
            for j in range(2):
                vstt(lamt[:, :], lamb[:, b0 + j * 128: b0 + j * 128 + 64], 1.0, lamb[:, b0 + j * 128 + 64: b0 + j * 128 + 128], ALU.mult, ALU.mult,
                     ["lamb"], ["lamt"])
                S.add("dve", (lambda j=j: (lambda e: e.tensor_reduce(out=lamc[:, 8 + j: 9 + j], in_=lamt[:, :], axis=mybir.AxisListType.X, op=ALU.add)))(),
                      ["lamt"], [f"lams{j}"])
                act(lamc[:, 10 + j: 11 + j], lamc[:, 8 + j: 9 + j], AF.Exp, [f"lams{j}"], [f"lame{j}"])
            vtt(lamc[:, 12:13], lamc[:, 10:11], lamc[:, 11:12], ALU.subtract, ["lame0", "lame1"], ["lamd"])
            vts(lamc[:, l: l + 1], lamc[:, 12:13], -1.0, -lam_init, ALU.mult, ALU.add, ["lamd"], ["lamc"])
            vts(subg[:, l, :], subg[:, l, :], 1.0 - lam_init, None, ALU.mult, None, ["subg"], ["subg"])
            dma(atoms[:].rearrange("p a c -> p (a c)"), atoms_in[l * 128:(l + 1) * 128, :], [], ["atoms"], q="cast")

        ntab_holder = {}
        btab_in = din("btab", [128, NTAB * 128])
        if "btab" not in dbg.get("skip", ()):
            dma(btab[:].rearrange("p a c -> p (a c)"), btab_in[:, :], [], ["btab"], q="cast")

        issue_casts(jobs_a(0))
        npass = dbg.get('npass', depth + 1)
        tiles_run = tiles[:dbg.get('ntiles', len(tiles))]
        for p in range(npass):
            nxt = []
            if p < depth:
                nxt = jobs_b(p) + (jobs_a(p + 1) if p + 1 < depth else [])
            per = (len(nxt) + len(tiles) - 1) // len(tiles) if nxt else 0
            if p >= 1:
                layer_setup(p - 1)
            for ti, t in enumerate(tiles_run if p == 0 else tiles[:dbg.get('ntiles1', len(tiles_run))]):
                issue_casts(nxt[ti * per:(ti + 1) * per])
                nb = len(t.blocks)
                if p == 0:
                    for bi, (off, bs) in enumerate(t.blocks):
                        if t.qb[bi] < 0:
                            dma(h_tm[:bs, bi, :], meta_in[:, :], [], [hb(bi)], q="sp")
                        else:
                            r = t.seq * SEQ + t.qb[bi] * 128
                            dma(h_tm[:bs, bi, :], x_in[r:r + 128, :], [], [hb(bi)], q="sp")
                else:
                    for bi, (off, bs) in enumerate(t.blocks):
                        dma(h_tm[:bs, bi, :], Hs[t.row0 + off: t.row0 + off + bs, :], [f"H{t.seq}_{t.idx}"], [hb(bi)], q="sp")
                if p >= 1:
                    l = p - 1; par = l % 2
                    load_q(t, par)
                    if 'attnA' in stages: attn_A(t, l, par)
                    if 'attnB' in stages: attn_BC(t, l, par, "B")
                    if 'attnC' in stages: attn_BC(t, l, par, "C")
                    if 'merge' in stages: merge_stage(t, l)
                    if 'ffn2' in stages: ffn_stage(t, l, 2)
                if p < depth:
                    if 'ffn1' in stages: ffn_stage(t, p, 1)
                    if 'proj' in stages: proj_stage(t, p, p % 2)
                if dbg.get('dump_h') and p == npass - 1:
                    for bi, (off, bs) in enumerate(t.blocks):
                        if t.qb[bi] >= 0:
                            r = t.seq * SEQ + t.qb[bi] * 128
                            dma(out_d[r:r + 128, :], h_tm[:bs, bi, :], [hb(bi)], [f"OUT{r}"], q="sp")
                    continue
                if p < depth:
                    for bi, (off, bs) in enumerate(t.blocks):
                        dma(Hs[t.row0 + off: t.row0 + off + bs, :], h_tm[:bs, bi, :], [hb(bi)], [f"H{t.seq}_{t.idx}"], q="sp")
                else:
                    if ti == 0:
                        dma(gfin, gfin_in.partition_broadcast(128), [], ["gfin", "vst0", "vst1"], q="sp")
                    for bi, (off, bs) in enumerate(t.blocks):
                        if t.qb[bi] < 0:
                            continue
                        act(xhat[:bs, bi % 2, :], h_tm[:bs, bi, :], AF.Square, [hb(bi)], [f"ss{bi}", f"xhat{bi % 2}"], accum=stat[:bs, bi:bi + 1])
                        act(stat[:bs, 8 + bi:9 + bi], stat[:bs, bi:bi + 1], AF.Sqrt, [f"ss{bi}"], [f"sd{bi}"], scale=1.0 / D, bias=EPS)
                        vrec(stat[:bs, 16 + bi:17 + bi], stat[:bs, 8 + bi:9 + bi], [f"sd{bi}"], [f"rs{bi}"])
                        vstt(h_tm[:bs, bi, :], h_tm[:bs, bi, :], stat[:bs, 16 + bi:17 + bi], gfin[:bs, :], ALU.mult, ALU.mult,
                             [hb(bi), f"rs{bi}", "gfin"], [hb(bi)])
                        r = t.seq * SEQ + t.qb[bi] * 128
                        dma(out_d[r:r + 128, :], h_tm[:bs, bi, :], [hb(bi)], [f"OUT{r}"], q="sp")
        assert len(tab.specs) <= NTAB, len(tab.specs)
        S.emit(nc, block, sems, lanes)
    return nc, tab


_CACHE = {}


def _get_program(nseq, depth):
    key = (nseq, depth)
    if key not in _CACHE:
        _CACHE[key] = build_program(nseq, depth)
    return _CACHE[key]


def prep_shared(inp, tab):
    f = np.float32
    sh = {}
    sh["meta"] = np.ascontiguousarray(inp["meta_tokens"], f)
    sh["w_ffn1_in"] = np.ascontiguousarray(inp["w_ffn1_in"], f).reshape(DEPTH * D, 2 * DFF)
    sh["w_ffn1_out"] = np.ascontiguousarray(inp["w_ffn1_out"], f).reshape(DEPTH * DFF, D)
    sh["w_ffn2_in"] = np.ascontiguousarray(inp["w_ffn2_in"], f).reshape(DEPTH * D, 2 * DFF)
    sh["w_ffn2_out"] = np.ascontiguousarray(inp["w_ffn2_out"], f).reshape(DEPTH * DFF, D)
    sh["w_in"] = np.ascontiguousarray(inp["w_in"], f).reshape(DEPTH * D, 3840)
    sh["w_gate"] = np.ascontiguousarray(inp["w_gate"], f).reshape(DEPTH * 3 * D, D)
    sh["w_branch"] = np.ascontiguousarray(inp["w_branch"], f).reshape(DEPTH * 3 * 512, D)
    sh["w_out"] = np.ascontiguousarray(inp["w_out"], f).reshape(DEPTH * D, D)
    g = np.stack([np.asarray(inp["norm_ffn1"], f), np.asarray(inp["norm_mix"], f), np.asarray(inp["norm_ffn2"], f)], axis=1)
    sh["gcols"] = np.ascontiguousarray(g.reshape(DEPTH * 3, 8, 128).transpose(2, 0, 1).reshape(128, DEPTH * 3 * 8))
    sh["final_norm"] = np.ascontiguousarray(inp["final_norm"], f).reshape(1, D)
    lam = np.stack([np.asarray(inp[k], f) for k in ("lambda_q1", "lambda_k1", "lambda_q2", "lambda_k2")], axis=1)
    sh["lam4"] = np.ascontiguousarray(lam.reshape(1, DEPTH * 4 * 64))
    sh["subln"] = np.ascontiguousarray(inp["subln_gain"], f).reshape(1, DEPTH * 128)
    sh["sink"] = np.ascontiguousarray(inp["sink_logits"], f).reshape(1, DEPTH * 8)
    t5 = np.asarray(inp["t5_table"], f)
    sh["t5flat"] = np.ascontiguousarray(t5.reshape(1, 384))
    sh["ident"] = np.eye(128, dtype=f)
    sel = np.zeros((128, 2, 128), f)
    for p in range(128):
        sel[p, p // 64, p] = 1.0
    sh["sel"] = sel.reshape(128, 256)
    sh["atoms"] = np.ascontiguousarray(build_atoms(np.asarray(inp["natten_rpb"], f)).reshape(DEPTH * 128, 122 * 64))
    bt = tab.fill(t5)
    full = np.zeros((128, 96, 128), f)
    full[:, :bt.shape[1], :] = bt
    sh["btab"] = full.reshape(128, 96 * 128)
    return sh


def kernel(**inputs):
    nseq = 2
    nc, tab = _get_program(nseq, DEPTH)
    sh = prep_shared(inputs, tab)
    x = np.ascontiguousarray(inputs["x"], np.float32)
    in_maps = []
    for c in range(8):
        m = dict(sh)
        m["x"] = x[c * nseq:(c + 1) * nseq].reshape(nseq * SEQ, D)
        in_maps.append(m)
    res = run_bass_kernel_spmd(nc, in_maps, core_ids=list(range(8)))
    out = np.concatenate([r["out"].reshape(nseq, SEQ, D) for r in res.results], axis=0)
    return out.astype(np.float32)
```

```python
import math
import numpy as np
import concourse.bass as bass
import concourse.mybir as mybir
from concourse.bass_utils import run_bass_kernel_spmd

F32 = mybir.dt.float32
BF16 = mybir.dt.bfloat16
AF = mybir.ActivationFunctionType
ALU = mybir.AluOpType

D = 1024; DFF = 2816; NM = 16; SEQ = 4096; L = SEQ + NM; DEPTH = 4
NVC = 4 * 129 + 8 * 65 + 2 * 65
VA0, VB0, VC0 = 0, 516, 1036
EPS = 1e-6
NEGM = -30000.0
NQK = 22


class Op:
    __slots__ = ("eng", "fn", "deps", "dma", "inc", "ms", "lane", "tgt", "lc")


class Sched:
    ENGS = ("pe", "act", "dve", "pool", "sp")

    def __init__(self):
        self.ops = {e: [] for e in self.ENGS}
        self.lastw = {}
        self.rd_eng = {}
        self.rd_dma = {}
        self.n = 0

    def add(self, eng, fn, reads=(), writes=(), dma=False):
        op = Op()
        op.lc = eng
        if eng == "cast":
            eng = "pool"
        op.eng = eng; op.fn = fn; op.dma = dma; op.inc = False; op.ms = 0; op.lane = 0; op.tgt = 0
        deps = []
        for r in reads:
            w = self.lastw.get(r)
            if w is not None:
                deps.append(w)
        for r in writes:
            w = self.lastw.get(r)
            if w is not None:
                deps.append(w)
            re = self.rd_eng.get(r)
            if re:
                deps.extend(re.values())
            rdm = self.rd_dma.get(r)
            if rdm:
                deps.extend(rdm)
        fd = []
        seen = set()
        for d in deps:
            if d is op or id(d) in seen:
                continue
            seen.add(id(d))
            if (not d.dma) and (not dma) and d.eng == "pe" and eng == "pe":
                continue
            if not d.dma:
                d.inc = True
            fd.append(d)
        op.deps = fd
        for r in reads:
            if dma:
                self.rd_dma.setdefault(r, []).append(op)
            else:
                self.rd_eng.setdefault(r, {})[eng] = op
        for r in writes:
            self.lastw[r] = op
            self.rd_eng[r] = {}
            self.rd_dma[r] = []
        self.ops[eng].append(op)
        self.n += 1
        return op

    def emit(self, nc, block, sems, lanes):
        for e in self.ENGS:
            c = 0
            for op in self.ops[e]:
                if op.inc and not op.dma:
                    c += 1
                    op.ms = c
        lane_cnt = {}
        li = {}
        for e in self.ENGS:
            for op in self.ops[e]:
                if op.dma:
                    ln = lanes[op.lc]
                    i = li.get(op.lc, 0)
                    li[op.lc] = i + 1
                    s = ln[i % len(ln)]
                    lane_cnt[s] = lane_cnt.get(s, 0) + 1
                    op.lane = s
                    op.tgt = 16 * lane_cnt[s]
        final = dict(lane_cnt)

        def run(e, eng):
            seen = {}

            def wait(sem, val):
                if seen.get(id(sem), 0) < val:
                    eng.wait_ge(sem, val)
                    seen[id(sem)] = val

            for op in self.ops[e]:
                for d in op.deps:
                    if d.dma:
                        wait(d.lane, d.tgt)
                    else:
                        wait(sems[d.eng], d.ms)
                if op.dma:
                    if op.tgt > 16:
                        wait(op.lane, op.tgt - 16)
                    op.fn(eng).then_inc(op.lane, 16)
                elif op.inc:
                    op.fn(eng).then_inc(sems[e], 1)
                else:
                    op.fn(eng)
            if e == "sp":
                for s, c in final.items():
                    wait(s, 16 * c)

        @block.tensor
        def _(eng):
            run("pe", eng)

        @block.scalar
        def _(eng):
            run("act", eng)

        @block.vector
        def _(eng):
            run("dve", eng)

        @block.gpsimd
        def _(eng):
            run("pool", eng)

        @block.sync
        def _(eng):
            run("sp", eng)


class Tile:
    pass


def make_tiles(nseq):
    tiles = []
    for s in range(nseq):
        for i in range(8):
            t = Tile()
            t.seq = s; t.idx = i
            if i == 0:
                t.pos0 = 0; t.n = 528
                t.blocks = [(0, 16)] + [(16 + 128 * j, 128) for j in range(4)]
                t.qb = [-1, 0, 1, 2, 3]
                t.chunks = [(0, 272), (272, 256)]
                t.groups = [[0, 1, 2], [3, 4]]
            else:
                t.pos0 = 16 + 512 * i; t.n = 512
                t.blocks = [(128 * j, 128) for j in range(4)]
                t.qb = [4 * i + j for j in range(4)]
                t.chunks = [(0, 512)]
                t.groups = [[0, 1, 2, 3]]
            t.row0 = s * L + t.pos0
            tiles.append(t)
    return tiles


def t5_bucket_np(rel):
    nb = 16; max_exact = 8
    ret = np.where(rel > 0, nb, 0)
    n = np.abs(rel)
    nf = np.maximum(n, 1).astype(np.float32)
    large = max_exact + (np.log(nf / max_exact) / np.float32(math.log(128 / max_exact))
                         * (nb - max_exact)).astype(np.int32)
    large = np.minimum(large, nb - 1)
    return ret + np.where(n < max_exact, n, large)


class BiasTab:
    def __init__(self):
        self.keys = {}
        self.specs = []

    def get(self, mixer, h, kpos0, nk, qpos0, nq):
        lo = kpos0 - (qpos0 + nq - 1); hi = (kpos0 + nk - 1) - qpos0
        kmeta = kpos0 < NM
        if lo >= 91:
            d = "P"
        elif hi <= -91:
            d = "N"
        else:
            d = kpos0 - qpos0
        if mixer == "C" and not kmeta:
            d = kpos0 - qpos0
        key = (mixer, h, nk, nq, d, kmeta if mixer == "C" else False)
        if key not in self.keys:
            self.keys[key] = len(self.specs)
            self.specs.append((mixer, h, kpos0, nk, qpos0, nq))
        return self.keys[key]

    def fill(self, t5):
        out = np.zeros((128, len(self.specs), 128), np.float32)
        for i, (mixer, h, kpos0, nk, qpos0, nq) in enumerate(self.specs):
            kp = kpos0 + np.arange(nk)[:, None]; qp = qpos0 + np.arange(nq)[None, :]
            rel = kp - qp
            b = t5_bucket_np(rel)
            if mixer == "A":
                v = t5[b, h]
            else:
                v = t5[b, 4 + h]
                vis = (kp < NM) | (np.abs(rel) <= 128)
                v = np.where(vis, v, np.float32(NEGM))
            out[:nk, i, :nq] = v
        return out


def build_atoms(rpb):
    cols = np.arange(64)
    cstart = np.clip(cols - 8, 0, 48)
    ok = (cols[None, :] >= cstart[:, None]) & (cols[None, :] < cstart[:, None] + 16)
    cidx = np.clip(cols[None, :] - cols[:, None] + 15, 0, 30)
    out = np.full((DEPTH, 128, 122, 64), np.float32(NEGM), np.float32)
    for l in range(DEPTH):
        for hd in range(8):
            for dr in range(15):
                g = rpb[l, hd, dr][cidx]
                a = np.where(ok, g, np.float32(NEGM)).T
                out[l, 0:64, hd * 15 + dr, :] = a
                out[l, 64:128, hd * 15 + dr, :] = a
        mq = np.full((64, 64), np.float32(NEGM), np.float32)
        mq[:16, :16] = 0.0
        out[l, 0:64, 121, :] = mq
        out[l, 64:128, 121, :] = mq
    return out


def build_program(nseq=2, depth=DEPTH, dbg=None):
    dbg = dbg or {}
    stages = dbg.get('stages', {'attnA', 'attnB', 'attnC', 'merge', 'ffn2', 'ffn1', 'proj'})
    nc = bass.Bass("TRN2", target_bir_lowering=False)
    S = Sched()
    tab = BiasTab()
    tiles = make_tiles(nseq)
    NTOK = nseq * L

    def din(name, shape, dt=F32):
        return nc.dram_tensor(name, list(shape), dt, kind="ExternalInput").ap()

    def dscr(name, shape, dt):
        return nc.dram_tensor(name, list(shape), dt, kind="Internal").ap()

    x_in = din("x", [nseq * SEQ, D])
    meta_in = din("meta", [NM, D])
    w_f1i = din("w_ffn1_in", [DEPTH * D, 2 * DFF]); w_f1o = din("w_ffn1_out", [DEPTH * DFF, D])
    w_f2i = din("w_ffn2_in", [DEPTH * D, 2 * DFF]); w_f2o = din("w_ffn2_out", [DEPTH * DFF, D])
    w_inp = din("w_in", [DEPTH * D, 3840]); w_gat = din("w_gate", [DEPTH * 3 * D, D])
    w_brn = din("w_branch", [DEPTH * 3 * 512, D]); w_out = din("w_out", [DEPTH * D, D])
    gcols_in = din("gcols", [128, 3 * DEPTH * 8])
    gfin_in = din("final_norm", [1, D])
    lam_in = din("lam4", [1, DEPTH * 4 * 64])
    subg_in = din("subln", [1, DEPTH * 128])
    sink_in = din("sink", [1, DEPTH * 8])
    t5_in = din("t5flat", [1, 384])
    ident_in = din("ident", [128, 128])
    sel_in = din("sel", [128, 256])
    atoms_in = din("atoms", [DEPTH * 128, 122 * 64])
    out_d = nc.dram_tensor("out", [nseq * SEQ, D], F32, kind="ExternalOutput").ap()

    b_f1i = dscr("b_f1i", [DEPTH * D, 2 * DFF], BF16); b_f1o = dscr("b_f1o", [DEPTH * DFF, D], BF16)
    b_f2i = dscr("b_f2i", [DEPTH * D, 2 * DFF], BF16); b_f2o = dscr("b_f2o", [DEPTH * DFF, D], BF16)
    b_inp = dscr("b_inp", [DEPTH * D, 3840], BF16); b_gat = dscr("b_gat", [DEPTH * 3 * D, D], BF16)
    b_brn = dscr("b_brn", [DEPTH * 3 * 512, D], BF16); b_out = dscr("b_out", [DEPTH * D, D], BF16)
    Hs = dscr("Hs", [NTOK, D], F32)
    QKs = dscr("QKs", [2 * nseq * NQK * 128, L], BF16)
    Vs = dscr("Vs", [2 * nseq * L, NVC], BF16)

    import contextlib
    es = contextlib.ExitStack()

    def sb(name, shape, dt):
        return es.enter_context(nc.sbuf_tensor("sb_" + name, list(shape), dt))

    NTAB = 96
    with es:
        h_tm = sb("h_tm", [128, 5, D], F32)
        junk = sb("junk", [128, 512], BF16)
        stat = sb("stat", [128, 64], F32)
        xhat = sb("xhat", [128, 2, D], BF16)
        xnT = sb("xnT", [128, 8, 528], BF16)
        big = sb("big", [128, 2 * 8448], BF16)
        wring = sb("wring", [128, 3, 4096], BF16)
        sg = sb("sg", [128, 2, 512], F32)
        qTa = sb("qTa", [128, 4, 528], BF16); qTb = sb("qTb", [128, 4, 528], BF16); qTc = sb("qTc", [128, 4, 528], BF16)
        kTb = sb("kTb", [128, 4, 16 + 1024], BF16); Vb = sb("Vb", [128, 9, 520], BF16)
        kTc = sb("kTc", [128, 2, 16 + 768], BF16); Vc = sb("Vc", [128, 7, 130], BF16)
        ptb = sb("ptb", [128, 3, 512], BF16)
        y_tm = sb("y_tm", [128, 5, 512], BF16)
        yT = sb("yT", [128, 12, 528], BF16)
        vst = sb("vst", [128, 2, NVC], BF16)
        o0n = sb("o0n", [128, 4, 128], F32)
        dA = sb("dA", [128, 4, 128], F32)
        ident = sb("ident", [128, 128], BF16)
        sel = sb("sel", [128, 2, 128], BF16)
        atoms = sb("atoms", [128, 122, 64], BF16)
        zt = sb("zt", [128, 128], BF16)
        gcols = sb("gcols", [128, 3 * DEPTH, 8], F32)
        lamb = sb("lamb", [128, 256], F32)
        btab = sb("btab", [128, NTAB, 128], BF16)
        lamt = sb("lamt", [128, 64], F32)
        lamc = sb("lamc", [128, 32], F32)
        subg = sb("subg", [128, DEPTH, 128], F32)
        sinkb = sb("sinkb", [128, DEPTH * 8], F32)
        t5b = sb("t5b", [128, 384], F32)

        ps = [es.enter_context(nc.psum_tensor(f"ps{i}", [128, 512], F32)) for i in range(6)]
        pst = [es.enter_context(nc.psum_tensor(f"pst{i}", [128, 1024], BF16)) for i in range(2)]
        sems = {e: es.enter_context(nc.semaphore(f"sem_{e}")) for e in Sched.ENGS}
        lanes = {"sp": [es.enter_context(nc.semaphore(f"lsp{i}")) for i in range(dbg.get("splanes", 8))],
                 "pool": [es.enter_context(nc.semaphore(f"lpl{i}")) for i in range(20)],
                 "cast": [es.enter_context(nc.semaphore(f"lca{i}")) for i in range(dbg.get("castlanes", 3))]}
        block = es.enter_context(nc.Block())

        hT = big[:, 0:22 * 528].rearrange("p (k n) -> p k n", k=22)
        mrg_acc = big[:, 0:4 * 528 * 2].bitcast(F32).rearrange("p (k n) -> p k n", k=4)
        mrgT = big[:, 4 * 528 * 2: 4 * 528 * 2 + 8 * 528].rearrange("p (k n) -> p k n", k=8)
        kva_k = [big[:, i * 8448: i * 8448 + L] for i in range(2)]
        kva_v = [big[:, i * 8448 + 4128: i * 8448 + 4128 + 33 * 129].rearrange("p (t c) -> p t c", c=129)
                 for i in range(2)]
        BIG = ["big", "kva0", "kva1"]
        qkst = big[:, 0:2 * 4 * 528].rearrange("p (a c n) -> p a c n", a=2, c=4)
        YTA = ["yT0", "yT1", "yT2"]
        gfin = vst[:, :, :].rearrange("p a n -> p (a n)")[:, 0:2 * D].bitcast(F32)

        def mm(out, lhsT, rhs, start, stop, R, W):
            S.add("pe", lambda e: e.matmul(out, lhsT, rhs, start=start, stop=stop, skip_group_check=True), R, W)

        def tr(out, in_, idn, R, W):
            S.add("pe", lambda e: e.transpose(out, in_, idn), R, W)

        def act(out, in_, func, R, W, bias=None, scale=None, accum=None):
            kw = {}
            if bias is not None: kw["bias"] = bias
            if scale is not None: kw["scale"] = scale
            if accum is not None: kw["accum_out"] = accum
            S.add("act", lambda e: e.activation(out=out, in_=in_, func=func, **kw), R, W)

        def dma(out, in_, R, W, q="sp"):
            S.add(q, lambda e: e.dma_start(out=out, in_=in_), R, W, dma=True)

        def dma_nc(out, in_, R, W, q="sp"):
            S.add(q, lambda e: e.dma_start(out=out, in_=in_, allow_slow_non_contiguous=True), R, W, dma=True)

        def vts(out, in0, s1, s2, op0, op1, R, W):
            if s2 is None:
                S.add("dve", lambda e: e.tensor_scalar(out=out, in0=in0, scalar1=s1, scalar2=None, op0=op0), R, W)
            else:
                S.add("dve", lambda e: e.tensor_scalar(out=out, in0=in0, scalar1=s1, scalar2=s2, op0=op0, op1=op1), R, W)

        def vtt(out, in0, in1, op, R, W):
            S.add("dve", lambda e: e.tensor_tensor(out=out, in0=in0, in1=in1, op=op), R, W)

        def vstt(out, in0, sc, in1, op0, op1, R, W):
            S.add("dve", lambda e: e.scalar_tensor_tensor(out=out, in0=in0, scalar=sc, in1=in1, op0=op0, op1=op1), R, W)

        def vrec(out, in_, R, W):
            S.add("dve", lambda e: e.reciprocal(out, in_), R, W)

        def vcopy(out, in_, R, W):
            S.add("dve", lambda e: e.tensor_copy(out, in_), R, W)

        def vmemset(ap, val, R, W):
            S.add("dve", lambda e: e.memset(ap, val), R, W)

        dma(ident[:], ident_in[:, :], [], ["ident"], q="cast")
        dma(sel[:].rearrange("p a m -> p (a m)"), sel_in[:, :], [], ["sel"], q="cast")
        dma(gcols[:].rearrange("p a k -> p (a k)"), gcols_in[:, :], [], ["gcols"])
        if "bcast" not in dbg.get("skip", ()):
            dma(subg[:].rearrange("p a k -> p (a k)"), subg_in.partition_broadcast(128), [], ["subg"])
            dma(sinkb[:], sink_in.partition_broadcast(128), [], ["sinkb"])
            dma(t5b[:], t5_in.partition_broadcast(128), [], ["t5b"])
        vmemset(vst[:], 1.0, [], ["vst0", "vst1"])
        vmemset(zt[:], 0.0, [], ["zt"])
        act(sinkb[:], sinkb[:], AF.Exp, ["sinkb"], ["sinkb"])

        WT = {}

        def cast_jobs(src, dst, rows, row0, name):
            jobs = []
            for r in range(0, rows, 128):
                rr = min(128, rows - r)
                jobs.append((dst[row0 + r: row0 + r + rr, :], src[row0 + r: row0 + r + rr, :], f"{name}_{r}"))
                WT.setdefault(name, []).append(f"{name}_{r}")
            return jobs

        def jobs_a(l):
            return (cast_jobs(w_f1i, b_f1i, D, l * D, f"Wf1i{l}") + cast_jobs(w_f1o, b_f1o, DFF, l * DFF, f"Wf1o{l}")
                    + cast_jobs(w_inp, b_inp, D, l * D, f"Winp{l}"))

        def jobs_b(l):
            return (cast_jobs(w_gat, b_gat, 3 * D, l * 3 * D, f"Wgat{l}") + cast_jobs(w_brn, b_brn, 3 * 512, l * 3 * 512, f"Wbrn{l}")
                    + cast_jobs(w_out, b_out, D, l * D, f"Wout{l}")
                    + cast_jobs(w_f2i, b_f2i, D, l * D, f"Wf2i{l}") + cast_jobs(w_f2o, b_f2o, DFF, l * DFF, f"Wf2o{l}"))

        def issue_casts(jobs):
            if 'casts' in dbg.get('skip', ()):
                return
            for (o, i, name) in jobs:
                dma(o, i, [], [name], q="cast")

        wr_state = {"i": 0}

        def wslot():
            i = wr_state["i"] % 3
            wr_state["i"] += 1
            return i

        def wload_cols(bw, row0, nk, c0, ncols, name):
            s = wslot()
            v = wring[:, s, 0:nk * ncols].rearrange("p (k c) -> p k c", k=nk)
            dma(v, bw[row0: row0 + nk * 128, c0: c0 + ncols].rearrange("(k p) c -> p k c", p=128), WT[name], [f"w{s}"])
            return s, v

        def hb(bi):
            return f"h{bi}"

        def norm_stage(t, gidx, filler=None):
            nb = len(t.blocks)
            for bi, (off, bs) in enumerate(t.blocks):
                act(xhat[:bs, bi % 2, :], h_tm[:bs, bi, :], AF.Square, [hb(bi)], [f"ss{bi}", f"xhat{bi % 2}"], accum=stat[:bs, bi:bi + 1])
            for bi, (off, bs) in enumerate(t.blocks):
                act(stat[:bs, 8 + bi:9 + bi], stat[:bs, bi:bi + 1], AF.Sqrt, [f"ss{bi}"], [f"sd{bi}"], scale=1.0 / D, bias=EPS)
            for bi, (off, bs) in enumerate(t.blocks):
                vrec(stat[:bs, 16 + bi:17 + bi], stat[:bs, 8 + bi:9 + bi], [f"sd{bi}"], [f"rs{bi}"])

            def xh(bi):
                off, bs = t.blocks[bi]
                xi = bi % 2
                if xi == 1:
                    act(xhat[:bs, xi, :], h_tm[:bs, bi, :], AF.Copy, [hb(bi), f"rs{bi}"], [f"xhat{xi}"], scale=stat[:bs, 16 + bi:17 + bi])
                else:
                    vts(xhat[:bs, xi, :], h_tm[:bs, bi, :], stat[:bs, 16 + bi:17 + bi], None, ALU.mult, None,
                        [hb(bi), f"rs{bi}"], [f"xhat{xi}"])
            pre = min(2, nb) if filler is not None else 0
            for bi in range(pre):
                xh(bi)
            if filler is not None:
                filler()
            for bi, (off, bs) in enumerate(t.blocks):
                xi = bi % 2
                if bi >= pre:
                    xh(bi)
                pv = pst[xi][:, :].rearrange("p (k n) -> p k n", k=8)
                for k in range(8):
                    tr(pv[:, k, :bs], xhat[:bs, xi, k * 128:(k + 1) * 128], ident[:bs, :bs],
                       [f"xhat{xi}", "ident"], [f"pst{xi}"])
                vtt(xnT[:, :, off:off + bs], pv[:, :, :bs], gcols[:, gidx, :].unsqueeze(2).to_broadcast([128, 8, bs]),
                    ALU.mult, [f"pst{xi}", "gcols"], [f"xn{bi}"])

        def xn_tokens(t):
            return [f"xn{bi}" for bi in range(len(t.blocks))]

        def ffn_stage(t, l, which, filler=None):
            bi_w, bo_w = (b_f1i, b_f1o) if which == 1 else (b_f2i, b_f2o)
            ni, no = (f"Wf1i{l}", f"Wf1o{l}") if which == 1 else (f"Wf2i{l}", f"Wf2o{l}")
            norm_stage(t, l * 3 + (0 if which == 1 else 2), filler)
            XN = xn_tokens(t)
            cnt = 0
            for j in range(11):
                s = wslot()
                v = wring[:, s, :].rearrange("p (k c) -> p k c", k=8)
                dma(v[:, :, 0:256], bi_w[l * D:(l + 1) * D, j * 256:(j + 1) * 256].rearrange("(k p) c -> p k c", p=128),
                    WT[ni], [f"w{s}"])
                dma(v[:, :, 256:512], bi_w[l * D:(l + 1) * D, DFF + j * 256: DFF + (j + 1) * 256].rearrange("(k p) c -> p k c", p=128),
                    WT[ni], [f"w{s}"])
                for c in range(2):
                    fc = j * 2 + c
                    for (off, cs) in t.chunks:
                        pg = ps[(cnt % 3) * 2]; pu = ps[(cnt % 3) * 2 + 1]
                        tg = f"ps{(cnt % 3) * 2}"; tu = f"ps{(cnt % 3) * 2 + 1}"
                        si = cnt % 2
                        hw = (BIG if cnt == 0 else []) + [f"hT{fc}"]
                        cnt += 1
                        for k in range(8):
                            mm(pg[:, :cs], v[:, k, c * 128:(c + 1) * 128], xnT[:, k, off:off + cs], k == 0, k == 7,
                               [f"w{s}"] + XN, [tg])
                        for k in range(8):
                            mm(pu[:, :cs], v[:, k, 256 + c * 128:256 + (c + 1) * 128], xnT[:, k, off:off + cs], k == 0, k == 7,
                               [f"w{s}"] + XN, [tu])
                        act(sg[:, si, :cs], pg[:, :cs], AF.Silu, [tg], [f"sg{si}"])
                        vtt(hT[:, fc, off:off + cs], sg[:, si, :cs], pu[:, :cs], ALU.mult, [f"sg{si}", tu], hw)
            nb = len(t.blocks)
            for half in range(2):
                banks = [1 + bi for bi in range(nb)]
                for (k0, nk) in ((0, 8), (8, 8), (16, 6)):
                    s, v = wload_cols(bo_w, l * DFF + k0 * 128, nk, half * 512, 512, no)
                    for bi, (off, bs) in enumerate(t.blocks):
                        for kk in range(nk):
                            k = k0 + kk
                            mm(ps[banks[bi]][:bs, :], hT[:, k, off:off + bs], v[:, kk, :], k == 0, k == 21,
                               [f"w{s}", f"hT{k}"] + BIG, [f"ps{banks[bi]}"])
                for bi, (off, bs) in enumerate(t.blocks):
                    vstt(h_tm[:bs, bi, half * 512:(half + 1) * 512], ps[banks[bi]][:bs, :], 0.5,
                         h_tm[:bs, bi, half * 512:(half + 1) * 512], ALU.mult, ALU.add, [f"ps{banks[bi]}", hb(bi)], [hb(bi)])

        def proj_stage(t, l, par, filler=None):
            norm_stage(t, l * 3 + 1, filler)
            XN = xn_tokens(t)
            nm = f"Winp{l}"
            qkbase = (par * nseq + t.seq) * NQK * 128
            fm = [(0, 4, 0.125, 0), (512, 4, None, 4), (1536, 4, 0.125, 8), (2048, 4, None, 12), (3072, 4, 0.125, 16)]
            cnt = 0
            for gi, (c0, nch, scl, ch0) in enumerate(fm + [(None, 2, None, 20)]):
                si = gi % 2
                if c0 is not None:
                    s, v = wload_cols(b_inp, l * D, 8, c0, 512, nm)
                else:
                    s = wslot()
                    v = wring[:, s, :].rearrange("p (k c) -> p k c", k=8)
                    src = b_inp[l * D:(l + 1) * D, :].rearrange("(k p) c -> p k c", p=128)
                    dma_nc(v[:, :, 0:128], src[:, :, 3584:3712], WT[nm], [f"w{s}"])
                    dma_nc(v[:, :, 128:192], src[:, :, 3648:3712], WT[nm], [f"w{s}"])
                    dma_nc(v[:, :, 192:256], src[:, :, 3584:3648], WT[nm], [f"w{s}"])
                    nch, scl, ch0 = 2, None, 20
                for c in range(nch):
                    for (off, cs) in t.chunks:
                        b = cnt % 4; cnt += 1
                        for k in range(8):
                            mm(ps[b][:, :cs], v[:, k, c * 128:(c + 1) * 128], xnT[:, k, off:off + cs], k == 0, k == 7,
                               [f"w{s}"] + XN, [f"ps{b}"])
                        if scl is not None:
                            act(qkst[:, si, c, off:off + cs], ps[b][:, :cs], AF.Copy, [f"ps{b}"], [f"qkst{si}"] + BIG, scale=scl)
                        else:
                            vcopy(qkst[:, si, c, off:off + cs], ps[b][:, :cs], [f"ps{b}"], [f"qkst{si}"] + BIG)
                dst = QKs[qkbase + ch0 * 128: qkbase + (ch0 + nch) * 128, t.pos0:t.pos0 + t.n].rearrange("(c p) n -> p c n", p=128)
                dma(dst, qkst[:, si, 0:nch, 0:t.n], [f"qkst{si}"] + BIG, [f"QK{par}_{t.seq}_{t.idx}"], q="sp")
            vsl = []
            for (c0, ncols) in ((1024, 512), (2560, 512)):
                vsl.append(wload_cols(b_inp, l * D, 8, c0, ncols, nm))
            s3, v3 = wload_cols(b_inp, l * D, 8, 3712, 128, nm)
            vbase = (par * nseq + t.seq) * L
            for bi, (off, bs) in enumerate(t.blocks):
                vi = bi % 2
                for gi2 in range(3):
                    b = 2 + (bi * 3 + gi2) % 4
                    if gi2 < 2:
                        s, v = vsl[gi2]; ncols = 512
                    else:
                        s, v = s3, v3; ncols = 128
                    for k in range(8):
                        mm(ps[b][:bs, :ncols], xnT[:, k, off:off + bs], v[:, k, :ncols], k == 0, k == 7,
                           [f"w{s}"] + XN, [f"ps{b}"])
                    if gi2 == 0:
                        dv = vst[:bs, vi, VA0:VA0 + 516].rearrange("p (h c) -> p h c", c=129)[:, :, 0:128]
                        sv = ps[b][:bs, :512].rearrange("p (h c) -> p h c", c=128)
                    elif gi2 == 1:
                        dv = vst[:bs, vi, VB0:VB0 + 520].rearrange("p (h c) -> p h c", c=65)[:, :, 0:64]
                        sv = ps[b][:bs, :512].rearrange("p (h c) -> p h c", c=64)
                    else:
                        dv = vst[:bs, vi, VC0:VC0 + 130].rearrange("p (h c) -> p h c", c=65)[:, :, 0:64]
                        sv = ps[b][:bs, :128].rearrange("p (h c) -> p h c", c=64)
                    if gi2 == 1:
                        S.add("act", (lambda dv=dv, sv=sv: (lambda e: e.copy(dv, sv)))(), [f"ps{b}"], [f"vst{vi}"])
                    else:
                        vcopy(dv, sv, [f"ps{b}"], [f"vst{vi}"])
                dma(Vs[vbase + t.pos0 + off: vbase + t.pos0 + off + bs, :], vst[:bs, vi, :], [f"vst{vi}"],
                    [f"V{par}_{t.seq}_{t.idx}"], q="sp")

        def seq_tokens(kind, par, seq):
            return [f"{kind}{par}_{seq}_{i}" for i in range(8)]

        def qpos_of(t, bi):
            return t.pos0 + t.blocks[bi][0]

        st_state = {"i": 0}

        def next_st():
            i = st_state["i"] % 2
            st_state["i"] += 1
            p = st_state["i"] % 3
            return i, p

        def transpose_y(t, branch):
            if dbg.get("dump_y") == branch:
                for bi, (off, bs) in enumerate(t.blocks):
                    vts(h_tm[:bs, bi, 0:512], y_tm[:bs, bi, :], 1.0, None, ALU.mult, None, [f"ytm{bi}", hb(bi)], [hb(bi)])
            for bi, (off, bs) in enumerate(t.blocks):
                xi = bi % 2
                pv = pst[xi][:, 0:512].rearrange("p (k n) -> p k n", k=4)
                for c in range(4):
                    tr(pv[:, c, :bs], y_tm[:bs, bi, c * 128:(c + 1) * 128], ident[:bs, :bs], [f"ytm{bi}", "ident"], [f"pst{xi}"])
                vts(yT[:, branch * 4:(branch + 1) * 4, off:off + bs], pv[:, :, :bs], 1.0, None, ALU.mult, None, [f"pst{xi}"], [f"yT{branch}"])

        def attn_A(t, l, par):
            qkbase = (par * nseq + t.seq) * NQK * 128
            vbase = (par * nseq + t.seq) * L
            QKT = seq_tokens("QK", par, t.seq); VT = seq_tokens("V", par, t.seq)
            lcol = lamc[:, l: l + 1]
            for h in range(4):
                ri = h % 2
                kk = kva_k[ri]; vv = kva_v[ri]
                r0 = qkbase + (4 + h) * 128
                KV = [f"kva{ri}"]
                dma(kk, QKs[r0:r0 + 128, :], QKT, KV, q="sp")
                dma_nc(vv[:16, 0, :], Vs[vbase: vbase + 16, VA0 + h * 129: VA0 + (h + 1) * 129], VT, KV, q="sp")
                for t4 in range(8):
                    dma_nc(vv[:, 1 + 4 * t4: 5 + 4 * t4, :],
                           Vs[vbase + 16 + 512 * t4: vbase + 16 + 512 * (t4 + 1), VA0 + h * 129: VA0 + (h + 1) * 129].rearrange("(t p) c -> p t c", p=128),
                           VT, KV, q="sp")
                for grp in (t.groups if dbg.get('a_mode', 2) >= 1 else []):
                    segs = [(bi, t.blocks[bi][0], t.blocks[bi][1]) for bi in grp]
                    g0 = segs[0][1]; gn = sum(sg_[2] for sg_ in segs)
                    order = sorted(range(len(segs)), key=lambda i: -segs[i][2])
                    for m in range(2):
                        ob = [2 + 2 * m, 3 + 2 * m]
                        first = {ob[0]: True, ob[1]: True}
                        pend = None
                        for kt in range(34):
                            cur = None
                            if kt < 33:
                                nk = 16 if kt == 0 else 128
                                kp0 = 0 if kt == 0 else 16 + 128 * (kt - 1)
                                sti, pti = next_st()
                                stb = ps[sti]
                                lo = kp0 - (t.pos0 + g0 + gn - 1); hi = (kp0 + nk - 1) - (t.pos0 + g0)
                                far = "P" if lo >= 91 else ("N" if hi <= -91 else None)
                                mm(stb[:nk, :gn], kk[m * 64:(m + 1) * 64, kp0:kp0 + nk], qTa[m * 64:(m + 1) * 64, h, g0:g0 + gn],
                                   True, far is not None, KV + ["qTa"], [f"ps{sti}"])
                                if far is None:
                                    for si_, (bi, off, bs) in enumerate(segs):
                                        ti = tab.get("A", h, kp0, nk, t.pos0 + off, bs)
                                        mm(stb[:nk, off - g0: off - g0 + bs], ident[:, :nk], btab[:, ti, :bs], False,
                                           si_ == len(segs) - 1, ["ident", "btab"], [f"ps{sti}"])
                                    act(ptb[:nk, pti, :gn], stb[:nk, :gn], AF.Exp, [f"ps{sti}"], [f"pt{pti}"])
                                else:
                                    col = (31 if far == "P" else 15) * 12 + h
                                    act(ptb[:nk, pti, :gn], stb[:nk, :gn], AF.Exp, [f"ps{sti}", "t5b"], [f"pt{pti}"],
                                        bias=t5b[:nk, col:col + 1])
                                cur = (kt, nk, pti)
                            if pend is not None and dbg.get('a_mode', 2) >= 1.5:
                                pkt, pnk, ppt = pend
                                for oi in order:
                                    bi, off, bs = segs[oi]
                                    bank = ob[oi // 2]; c0 = (oi % 2) * 129
                                    mm(ps[bank][:bs, c0:c0 + 129], ptb[:pnk, ppt, off - g0: off - g0 + bs], vv[:pnk, pkt, :],
                                       first[bank], pkt == 32, [f"pt{ppt}"] + KV, [f"ps{bank}"])
                                    first[bank] = False
                            pend = cur
                        if dbg.get('a_mode', 2) < 1.6:
                            continue
                        for oi, (bi, off, bs) in enumerate(segs):
                            bank = ob[oi // 2]; c0 = (oi % 2) * 129
                            vrec(stat[:bs, 24 + m * 4 + oi: 25 + m * 4 + oi], ps[bank][:bs, c0 + 128:c0 + 129], [f"ps{bank}"], [f"rz{m}_{oi}"])
                        if m == 0:
                            for oi, (bi, off, bs) in enumerate(segs):
                                bank = ob[oi // 2]; c0 = (oi % 2) * 129
                                vts(o0n[:bs, oi, :], ps[bank][:bs, c0:c0 + 128], stat[:bs, 24 + oi:25 + oi], None, ALU.mult, None,
                                    [f"ps{bank}", f"rz0_{oi}"], [f"o0n{oi}"])
                        else:
                            for oi, (bi, off, bs) in enumerate(segs):
                                vts(stat[:bs, 32 + oi:33 + oi], stat[:bs, 28 + oi:29 + oi], lcol[:bs, :], None, ALU.mult, None,
                                    [f"rz1_{oi}", "lamc"], [f"c1_{oi}"])
                            for oi, (bi, off, bs) in enumerate(segs):
                                bank = ob[oi // 2]; c0 = (oi % 2) * 129
                                vstt(dA[:bs, oi, :], ps[bank][:bs, c0:c0 + 128], stat[:bs, 32 + oi:33 + oi], o0n[:bs, oi, :],
                                     ALU.mult, ALU.add, [f"ps{bank}", f"c1_{oi}", f"o0n{oi}"], [f"dA{oi}"])
                    if dbg.get('a_mode', 2) < 1.8:
                        continue
                    for oi, (bi, off, bs) in enumerate(segs):
                        act(junk[:bs, oi * 128:(oi + 1) * 128], dA[:bs, oi, :], AF.Square, [f"dA{oi}"], [f"ssA{oi}", f"junk{oi}"], accum=stat[:bs, 36 + oi:37 + oi])
                    for oi, (bi, off, bs) in enumerate(segs):
                        act(stat[:bs, 40 + oi:41 + oi], stat[:bs, 36 + oi:37 + oi], AF.Sqrt, [f"ssA{oi}"], [f"sdA{oi}"], scale=1.0 / 128, bias=EPS)
                    for oi, (bi, off, bs) in enumerate(segs):
                        vrec(stat[:bs, 44 + oi:45 + oi], stat[:bs, 40 + oi:41 + oi], [f"sdA{oi}"], [f"rsA{oi}"])
                    for oi, (bi, off, bs) in enumerate(segs):
                        vstt(y_tm[:bs, bi, h * 128:(h + 1) * 128], dA[:bs, oi, :], stat[:bs, 44 + oi:45 + oi], subg[:bs, l, :],
                             ALU.mult, ALU.mult, [f"dA{oi}", f"rsA{oi}", "subg"], [f"ytm{bi}"])
            if dbg.get('a_mode', 2) >= 2:
                transpose_y(t, 0)

        def attn_BC(t, l, par, mixer):
            qkbase = (par * nseq + t.seq) * NQK * 128
            vbase = (par * nseq + t.seq) * L
            QKT = seq_tokens("QK", par, t.seq); VT = seq_tokens("V", par, t.seq)
            if mixer == "B":
                r0 = 8 * t.idx
                rlo = max(0, r0 - 4); rhi = min(64, r0 + 12)
                nrow = rhi - rlo
                src = QKs[qkbase + 12 * 128: qkbase + 16 * 128, :].rearrange("(c p) n -> p c n", p=128)
                dma(kTb[:, :, 0:16], src[:, :, 0:16], QKT, ["kTb"], q="sp")
                for c4 in range(4):
                    dma(kTb[:, c4, 16:16 + nrow * 64], src[:, c4, 16 + rlo * 64: 16 + rhi * 64], QKT, ["kTb"], q="sp")
                dma(Vb[:16, 0, :], Vs[vbase: vbase + 16, VB0:VB0 + 520], VT, ["Vb"], q="sp")
                for t2 in range(0, nrow // 2, 2):
                    dma(Vb[:, 1 + t2: 3 + t2, :], Vs[vbase + 16 + rlo * 64 + t2 * 128: vbase + 16 + rlo * 64 + (t2 + 2) * 128, VB0:VB0 + 520].rearrange("(t p) c -> p t c", p=128),
                        VT, ["Vb"], q="sp")
                qT = qTb; kT = kTb; Vt = Vb; KR = ["kTb", "qTb"]; VR = ["Vb"]
            else:
                j0 = t.qb[1] if t.idx == 0 else t.qb[0]
                jlo = max(0, j0 - 1); jhi = min(32, j0 + 5)
                nblk = jhi - jlo
                src = QKs[qkbase + 20 * 128: qkbase + 22 * 128, :].rearrange("(c p) n -> p c n", p=128)
                dma(kTc[:, :, 0:16], src[:, :, 0:16], QKT, ["kTc"], q="sp")
                dma(kTc[:, :, 16:16 + nblk * 128], src[:, :, 16 + jlo * 128: 16 + jhi * 128], QKT, ["kTc"], q="sp")
                dma(Vc[:16, 0, :], Vs[vbase: vbase + 16, VC0:VC0 + 130], VT, ["Vc"], q="sp")
                for t2 in range(0, nblk, 2):
                    n2 = min(2, nblk - t2)
                    dma_nc(Vc[:, 1 + t2: 1 + t2 + n2, :], Vs[vbase + 16 + (jlo + t2) * 128: vbase + 16 + (jlo + t2 + n2) * 128, VC0:VC0 + 130].rearrange("(t p) c -> p t c", p=128),
                           VT, ["Vc"], q="sp")
                qT = qTc; kT = kTc; Vt = Vc; KR = ["kTc", "qTc"]; VR = ["Vc"]
            job = 0
            for bi, (off, bs) in enumerate(t.blocks if dbg.get("b_mode", 2) >= 1 else []):
                jb = t.qb[bi]
                if "b_blocks" in dbg and bi not in dbg["b_blocks"]:
                    continue
                kts = [(16, 0, 0, ("meta",))]
                if mixer == "B":
                    if jb < 0:
                        rows = [(0, "mq"), (0, "mq")]
                        tl = range(0, 4)
                    else:
                        r = 2 * jb
                        rs0 = min(max(r - 4, 0), 56); rs1 = min(max(r + 1 - 4, 0), 56)
                        tl = range(rs0 // 2, (rs1 + 7) // 2 + 1)
                    for tt in tl:
                        kts.append((128, 16 + (2 * tt - rlo) * 64, 1 + (2 * tt - rlo) // 2, ("rows", tt)))
                else:
                    if jb < 0:
                        bl = [0]
                    else:
                        bl = [j for j in (jb - 1, jb, jb + 1) if 0 <= j < 32]
                    for j in bl:
                        kts.append((128, 16 + (j - jlo) * 128, 1 + (j - jlo), ("blk", j)))
                if "b_maxkt" in dbg:
                    kts = kts[:dbg["b_maxkt"]]
                if "b_kts" in dbg:
                    kts = [kts[i] for i in dbg["b_kts"] if i < len(kts)]
                for hg in range(dbg.get("b_nhg", 2)):
                    ob = 2 + (job % 4); job += 1
                    first = True
                    pend = None
                    for ki in range(len(kts) + 1):
                        cur = None
                        if ki < len(kts):
                            nk, kc0, vti, kinfo = kts[ki]
                            sti, pti = next_st()
                            stb = ps[sti]
                            firstst = True
                            for s_ in range(4):
                                hd = hg * 4 + s_
                                ch = hd // 2; half = hd % 2
                                if mixer == "B":
                                    ksel = kT[half * 64:(half + 1) * 64, ch, kc0:kc0 + nk]
                                else:
                                    kv = hg
                                    ksel = kT[half * 64:(half + 1) * 64, 0 if half == kv else 1, kc0:kc0 + nk]
                                nbias = 0
                                blist = []
                                if mixer == "B" and kinfo[0] == "meta":
                                    blist.append((ident[:, :nk], zt[:, :bs], s_ * 128, bs))
                                if mixer == "B" and kinfo[0] == "rows":
                                    tt = kinfo[1]
                                    if jb < 0:
                                        for a in range(2):
                                            blist.append((sel[:, a, :], atoms[:, 121, 0:16], s_ * 128, 16))
                                    else:
                                        r = 2 * jb
                                        for a in range(2):
                                            for b_ in range(2):
                                                rq = r + b_; kr = 2 * tt + a
                                                rs = min(max(rq - 4, 0), 56)
                                                vis = rs <= kr < rs + 8
                                                ai = hd * 15 + (kr - rq + 7) if vis else 120
                                                blist.append((sel[:, a, :], atoms[:, ai, :], s_ * 128 + b_ * 64, 64))
                                if mixer == "C" and not dbg.get("c_nobias"):
                                    kp0 = 0 if kinfo[0] == "meta" else 16 + 128 * kinfo[1]
                                    ti = tab.get("C", hd, kp0, nk, t.pos0 + off, bs)
                                    blist.append((ident[:, :nk], btab[:, ti, :bs], s_ * 128, bs))
                                mm(stb[:nk, s_ * 128: s_ * 128 + bs], ksel, qT[half * 64:(half + 1) * 64, ch, off:off + bs],
                                   firstst, len(blist) == 0 and s_ == 3, KR, [f"ps{sti}"])
                                firstst = False
                                for bj, (lh, rh, cc0, cn) in enumerate(blist):
                                    mm(stb[:nk, cc0:cc0 + cn], lh, rh, False, (s_ == 3 and bj == len(blist) - 1),
                                       ["sel", "atoms", "ident", "btab", "zt"], [f"ps{sti}"])
                            if bs == 128:
                                act(ptb[:nk, pti, :], stb[:nk, :], AF.Exp, [f"ps{sti}"], [f"pt{pti}"])
                            else:
                                act(ptb[:nk, pti, :].rearrange("p (s c) -> p s c", c=128)[:, :, :bs],
                                    stb[:nk, :].rearrange("p (s c) -> p s c", c=128)[:, :, :bs], AF.Exp, [f"ps{sti}"], [f"pt{pti}"])
                            cur = (nk, vti, pti)
                        if pend is not None and dbg.get("b_mode", 2) >= 1.5:
                            pnk, pvti, ppt = pend
                            for s_ in range(4):
                                hd = hg * 4 + s_
                                vcol = hd * 65 if mixer == "B" else hg * 65
                                mm(ps[ob][:bs, s_ * 65:(s_ + 1) * 65], ptb[:pnk, ppt, s_ * 128: s_ * 128 + bs], Vt[:pnk, pvti, vcol:vcol + 65],
                                   first, ki == len(kts) and s_ == 3, [f"pt{ppt}"] + VR, [f"ps{ob}"])
                                first = False
                        pend = cur
                    if dbg.get("b_mode", 2) < 2:
                        continue
                    ov = ps[ob][:bs, 0:260].rearrange("p (s c) -> p s c", c=65)
                    zc = 48 + hg * 4
                    if mixer == "C":
                        vtt(stat[:bs, zc:zc + 4], ov[:, :, 64], sinkb[:bs, l * 8 + hg * 4: l * 8 + hg * 4 + 4], ALU.add,
                            [f"ps{ob}", "sinkb"], [f"zc{hg}"])
                        vrec(stat[:bs, zc + 8:zc + 12], stat[:bs, zc:zc + 4], [f"zc{hg}"], [f"rzb{hg}"])
                    else:
                        vts(stat[:bs, zc:zc + 4], ov[:, :, 64], 1.0, None, ALU.mult, None, [f"ps{ob}"], [f"zc{hg}"])
                        vrec(stat[:bs, zc + 8:zc + 12], stat[:bs, zc:zc + 4], [f"zc{hg}"], [f"rzb{hg}"])
                    vtt(y_tm[:bs, bi, hg * 256:(hg + 1) * 256].rearrange("p (s c) -> p s c", c=64), ov[:, :, 0:64],
                        stat[:bs, zc + 8:zc + 12].unsqueeze(2).to_broadcast([bs, 4, 64]), ALU.mult, [f"ps{ob}", f"rzb{hg}"], [f"ytm{bi}"])
            if dbg.get("b_mode", 2) >= 2:
                transpose_y(t, 1 if mixer == "B" else 2)

        def load_q(t, par):
            qkbase = (par * nseq + t.seq) * NQK * 128
            for (buf, ch0, nm) in ((qTa, 0, "qTa"), (qTb, 8, "qTb"), (qTc, 16, "qTc")):
                src = QKs[qkbase + ch0 * 128: qkbase + (ch0 + 4) * 128, t.pos0:t.pos0 + t.n].rearrange("(c p) n -> p c n", p=128)
                dma(buf[:, :, 0:t.n], src, [f"QK{par}_{t.seq}_{t.idx}"], [nm], q="sp")

        def merge_stage(t, l):
            norm_stage(t, l * 3 + 1)
            XN = xn_tokens(t)
            cnt = 0
            for ch in range(2):
                for i in range(3):
                    sgt, vg = wload_cols(b_gat, (l * 3 + i) * D, 8, ch * 512, 512, f"Wgat{l}")
                    sbt, vb = wload_cols(b_brn, (l * 3 + i) * 512, 4, ch * 512, 512, f"Wbrn{l}")
                    for c in range(4):
                        for (off, cs) in t.chunks:
                            b = (cnt % 3) * 2; cnt += 1
                            si = (cnt % 2)
                            for k in range(8):
                                mm(ps[b][:, :cs], vg[:, k, c * 128:(c + 1) * 128], xnT[:, k, off:off + cs], k == 0, k == 7,
                                   [f"w{sgt}"] + XN, [f"ps{b}"])
                            for k in range(4):
                                mm(ps[b + 1][:, :cs], vb[:, k, c * 128:(c + 1) * 128], yT[:, i * 4 + k, off:off + cs], k == 0, k == 3,
                                   [f"w{sbt}", f"yT{i}"], [f"ps{b + 1}"])
                            act(sg[:, si, :cs], ps[b][:, :cs], AF.Sigmoid, [f"ps{b}"], [f"sg{si}"])
                            if i == 0:
                                vtt(mrg_acc[:, c, off:off + cs], sg[:, si, :cs], ps[b + 1][:, :cs], ALU.mult, [f"sg{si}", f"ps{b + 1}"], BIG)
                            else:
                                vtt(sg[:, si, :cs], sg[:, si, :cs], ps[b + 1][:, :cs], ALU.mult, [f"sg{si}", f"ps{b + 1}"], [f"sg{si}"])
                                if i == 1:
                                    vtt(mrg_acc[:, c, off:off + cs], mrg_acc[:, c, off:off + cs], sg[:, si, :cs], ALU.add, [f"sg{si}"] + BIG, BIG)
                                else:
                                    vtt(mrgT[:, ch * 4 + c, off:off + cs], mrg_acc[:, c, off:off + cs], sg[:, si, :cs], ALU.add, [f"sg{si}"] + BIG, BIG)
            for half in range(2):
                s, v = wload_cols(b_out, l * D, 8, half * 512, 512, f"Wout{l}")
                for bi, (off, bs) in enumerate(t.blocks):
                    b = 1 + bi
                    for k in range(8):
                        mm(ps[b][:bs, :], mrgT[:, k, off:off + bs], v[:, k, :], k == 0, k == 7, [f"w{s}"] + BIG, [f"ps{b}"])
                    vtt(h_tm[:bs, bi, half * 512:(half + 1) * 512], ps[b][:bs, :], h_tm[:bs, bi, half * 512:(half + 1) * 512], ALU.add,
                        [f"ps{b}", hb(bi)], [hb(bi)])

        def layer_setup(l):
            lam_init = 0.8 - 0.6 * math.exp(-0.3 * l)
            b0 = 0
            dma(lamb[:], lam_in[:, l * 256:(l + 1) * 256].partition_broadcast(128), [], ["lamb"])
            for j in range(2):
                vstt(lamt[:, :], lamb[:, b0 + j * 128: b0 + j * 128 + 64], 1.0, lamb[:, b0 + j * 128 + 64: b0 + j * 128 + 128], ALU.mult, ALU.mult,
                     ["lamb"], ["lamt"])
                S.add("dve", (lambda j=j: (lambda e: e.tensor_reduce(out=lamc[:, 8 + j: 9 + j], in_=lamt[:, :], axis=mybir.AxisListType.X, op=ALU.add)))(),
                      ["lamt"], [f"lams{j}"])
                act(lamc[:, 10 + j: 11 + j], lamc[:, 8 + j: 9 + j], AF.Exp, [f"lams{j}"], [f"lame{j}"])
            vtt(lamc[:, 12:13], lamc[:, 10:11], lamc[:, 11:12], ALU.subtract, ["lame0", "lame1"], ["lamd"])
            vts(lamc[:, l: l + 1], lamc[:, 12:13], -1.0, -lam_init, ALU.mult, ALU.add, ["lamd"], ["lamc"])
            vts(subg[:, l, :], subg[:, l, :], 1.0 - lam_init, None, ALU.mult, None, ["subg"], ["subg"])
            dma(atoms[:].rearrange("p a c -> p (a c)"), atoms_in[l * 128:(l + 1) * 128, :], [], ["atoms"], q="cast")

        ntab_holder = {}
        btab_in = din("btab", [128, NTAB * 128])
        if "btab" not in dbg.get("skip", ()):
            dma(btab[:].rearrange("p a c -> p (a c)"), btab_in[:, :], [], ["btab"], q="cast")

        issue_casts(jobs_a(0))
        npass = dbg.get('npass', depth + 1)
        tiles_run = tiles[:dbg.get('ntiles', len(tiles))]
        for p in range(npass):
            nxt = []
            if p < depth:
                nxt = jobs_b(p) + (jobs_a(p + 1) if p + 1 < depth else [])
            per = (len(nxt) + len(tiles) - 1) // len(tiles) if nxt else 0
            if p >= 1:
                layer_setup(p - 1)
            for ti, t in enumerate(tiles_run if p == 0 else tiles[:dbg.get('ntiles1', len(tiles_run))]):
                issue_casts(nxt[ti * per:(ti + 1) * per])
                nb = len(t.blocks)
                if p == 0:
                    for bi, (off, bs) in enumerate(t.blocks):
                        if t.qb[bi] < 0:
                            dma(h_tm[:bs, bi, :], meta_in[:, :], [], [hb(bi)], q="sp")
                        else:
                            r = t.seq * SEQ + t.qb[bi] * 128
                            dma(h_tm[:bs, bi, :], x_in[r:r + 128, :], [], [hb(bi)], q="sp")
                else:
                    for bi, (off, bs) in enumerate(t.blocks):
                        dma(h_tm[:bs, bi, :], Hs[t.row0 + off: t.row0 + off + bs, :], [f"H{t.seq}_{t.idx}"], [hb(bi)], q="sp")
                tl_ = tiles_run if p == 0 else tiles[:dbg.get('ntiles1', len(tiles_run))]
                fB = fC = None
                if p >= 1:
                    l = p - 1; par = l % 2
                    if ti == 0:
                        load_q(t, par)
                        if 'attnB' in stages: attn_BC(t, l, par, "B")
                        if 'attnC' in stages: attn_BC(t, l, par, "C")
                    if 'attnA' in stages: attn_A(t, l, par)
                    if ti + 1 < len(tl_):
                        tn = tl_[ti + 1]

                        def fB(tn=tn, l=l, par=par):
                            load_q(tn, par)
                            if 'attnB' in stages: attn_BC(tn, l, par, "B")

                        def fC(tn=tn, l=l, par=par):
                            if 'attnC' in stages: attn_BC(tn, l, par, "C")
                    if 'merge' in stages: merge_stage(t, l)
                    if p < depth:
                        if 'ffn2' in stages: ffn_stage(t, l, 2)
                    else:
                        def fBC(fB=fB, fC=fC):
                            if fB is not None:
                                fB(); fC()
                        if 'ffn2' in stages: ffn_stage(t, l, 2, fBC)
                if p < depth:
                    if 'ffn1' in stages: ffn_stage(t, p, 1, fB)
                    if 'proj' in stages: proj_stage(t, p, p % 2, fC)
                if dbg.get('dump_h') and p == npass - 1:
                    for bi, (off, bs) in enumerate(t.blocks):
                        if t.qb[bi] >= 0:
                            r = t.seq * SEQ + t.qb[bi] * 128
                            dma(out_d[r:r + 128, :], h_tm[:bs, bi, :], [hb(bi)], [f"OUT{r}"], q="sp")
                    continue
                if p < depth:
                    for bi, (off, bs) in enumerate(t.blocks):
                        dma(Hs[t.row0 + off: t.row0 + off + bs, :], h_tm[:bs, bi, :], [hb(bi)], [f"H{t.seq}_{t.idx}"], q="sp")
                else:
                    if ti == 0:
                        dma(gfin, gfin_in.partition_broadcast(128), [], ["gfin", "vst0", "vst1"], q="sp")
                    for bi, (off, bs) in enumerate(t.blocks):
                        if t.qb[bi] < 0:
                            continue
                        act(xhat[:bs, bi % 2, :], h_tm[:bs, bi, :], AF.Square, [hb(bi)], [f"ss{bi}", f"xhat{bi % 2}"], accum=stat[:bs, bi:bi + 1])
                        act(stat[:bs, 8 + bi:9 + bi], stat[:bs, bi:bi + 1], AF.Sqrt, [f"ss{bi}"], [f"sd{bi}"], scale=1.0 / D, bias=EPS)
                        vrec(stat[:bs, 16 + bi:17 + bi], stat[:bs, 8 + bi:9 + bi], [f"sd{bi}"], [f"rs{bi}"])
                        vstt(h_tm[:bs, bi, :], h_tm[:bs, bi, :], stat[:bs, 16 + bi:17 + bi], gfin[:bs, :], ALU.mult, ALU.mult,
                             [hb(bi), f"rs{bi}", "gfin"], [hb(bi)])
                        r = t.seq * SEQ + t.qb[bi] * 128
                        dma(out_d[r:r + 128, :], h_tm[:bs, bi, :], [hb(bi)], [f"OUT{r}"], q="sp")
        assert len(tab.specs) <= NTAB, len(tab.specs)
        S.emit(nc, block, sems, lanes)
    return nc, tab


_CACHE = {}


def _get_program(nseq, depth):
    key = (nseq, depth)
    if key not in _CACHE:
        _CACHE[key] = build_program(nseq, depth)
    return _CACHE[key]


def prep_shared(inp, tab):
    f = np.float32
    sh = {}
    sh["meta"] = np.ascontiguousarray(inp["meta_tokens"], f)
    sh["w_ffn1_in"] = np.ascontiguousarray(inp["w_ffn1_in"], f).reshape(DEPTH * D, 2 * DFF)
    sh["w_ffn1_out"] = np.ascontiguousarray(inp["w_ffn1_out"], f).reshape(DEPTH * DFF, D)
    sh["w_ffn2_in"] = np.ascontiguousarray(inp["w_ffn2_in"], f).reshape(DEPTH * D, 2 * DFF)
    sh["w_ffn2_out"] = np.ascontiguousarray(inp["w_ffn2_out"], f).reshape(DEPTH * DFF, D)
    sh["w_in"] = np.ascontiguousarray(inp["w_in"], f).reshape(DEPTH * D, 3840)
    sh["w_gate"] = np.ascontiguousarray(inp["w_gate"], f).reshape(DEPTH * 3 * D, D)
    sh["w_branch"] = np.ascontiguousarray(inp["w_branch"], f).reshape(DEPTH * 3 * 512, D)
    sh["w_out"] = np.ascontiguousarray(inp["w_out"], f).reshape(DEPTH * D, D)
    g = np.stack([np.asarray(inp["norm_ffn1"], f), np.asarray(inp["norm_mix"], f), np.asarray(inp["norm_ffn2"], f)], axis=1)
    sh["gcols"] = np.ascontiguousarray(g.reshape(DEPTH * 3, 8, 128).transpose(2, 0, 1).reshape(128, DEPTH * 3 * 8))
    sh["final_norm"] = np.ascontiguousarray(inp["final_norm"], f).reshape(1, D)
    lam = np.stack([np.asarray(inp[k], f) for k in ("lambda_q1", "lambda_k1", "lambda_q2", "lambda_k2")], axis=1)
    sh["lam4"] = np.ascontiguousarray(lam.reshape(1, DEPTH * 4 * 64))
    sh["subln"] = np.ascontiguousarray(inp["subln_gain"], f).reshape(1, DEPTH * 128)
    sh["sink"] = np.ascontiguousarray(inp["sink_logits"], f).reshape(1, DEPTH * 8)
    t5 = np.asarray(inp["t5_table"], f)
    sh["t5flat"] = np.ascontiguousarray(t5.reshape(1, 384))
    sh["ident"] = np.eye(128, dtype=f)
    sel = np.zeros((128, 2, 128), f)
    for p in range(128):
        sel[p, p // 64, p] = 1.0
    sh["sel"] = sel.reshape(128, 256)
    sh["atoms"] = np.ascontiguousarray(build_atoms(np.asarray(inp["natten_rpb"], f)).reshape(DEPTH * 128, 122 * 64))
    bt = tab.fill(t5)
    full = np.zeros((128, 96, 128), f)
    full[:, :bt.shape[1], :] = bt
    sh["btab"] = full.reshape(128, 96 * 128)
    return sh


def kernel(**inputs):
    nseq = 2
    nc, tab = _get_program(nseq, DEPTH)
    sh = prep_shared(inputs, tab)
    x = np.ascontiguousarray(inputs["x"], np.float32)
    in_maps = []
    for c in range(8):
        m = dict(sh)
        m["x"] = x[c * nseq:(c + 1) * nseq].reshape(nseq * SEQ, D)
        in_maps.append(m)
    res = run_bass_kernel_spmd(nc, in_maps, core_ids=list(range(8)))
    out = np.concatenate([r["out"].reshape(nseq, SEQ, D) for r in res.results], axis=0)
    return out.astype(np.float32)
```
